# Optimizing a Trainium2 kernel written in Bass

```python
import math
import jax, jax.numpy as jnp
from jax import lax
import numpy as np

D_MODEL = 1024
BATCH = 32
SEQ = 2048
DEPTH = 2

HEAD_DIM = 64
NSA_HEADS = 4
DIFF_HEADS = 4
FOX_HEADS = 8
NSA_WIDTH = NSA_HEADS * HEAD_DIM
DIFF_WIDTH = DIFF_HEADS * HEAD_DIM
FOX_WIDTH = FOX_HEADS * HEAD_DIM
MIX_WIDTH = NSA_WIDTH + DIFF_WIDTH + FOX_WIDTH
NSA_KV_HEADS = 1
NSA_GROUP = NSA_HEADS // NSA_KV_HEADS
NSA_BRANCHES = 3
CMP_BLOCK = 32
CMP_STRIDE = 16
CMP_HIDDEN = 256
SLC_BLOCK = 64
SLC_TOPK = 16
WINDOW = 512
DIFF_QK_DIM = HEAD_DIM // 2
DIFF_V_DIM = HEAD_DIM
D_FF = 2752
PLE_DIM = 256
ROPE_THETA = 10000.0
QUERY_BLOCK = 128
SLC_QUERY_BLOCK = 32
LN_EPS = 1e-5
NEG_BIG = -1e30
DEEPNORM_ALPHA = (2.0 * DEPTH) ** 0.25
DEEPNORM_BETA = (8.0 * DEPTH) ** -0.25
IN_SPLITS = (
    NSA_WIDTH,
    6 * NSA_KV_HEADS * HEAD_DIM,
    NSA_BRANCHES * NSA_HEADS,
    2 * DIFF_HEADS * DIFF_QK_DIM,
    2 * DIFF_HEADS * DIFF_QK_DIM,
    DIFF_HEADS * DIFF_V_DIM,
    FOX_WIDTH, FOX_WIDTH, FOX_WIDTH,
    FOX_HEADS,
)
IN_COLS = sum(IN_SPLITS)

kernel_name = "hymba_nsa_diff_fox_macaron_deepnorm"


def layer_norm(x, g, b):
    xf = x.astype(jnp.float32)
    mu = jnp.mean(xf, axis=-1, keepdims=True)
    var = jnp.mean(jnp.square(xf - mu), axis=-1, keepdims=True)
    return ((xf - mu) * lax.rsqrt(var + LN_EPS) * g.astype(jnp.float32) + b.astype(jnp.float32)).astype(x.dtype)


def rope(x, pos):
    d = x.shape[-1]
    inv = ROPE_THETA ** (-jnp.arange(0, d, 2, dtype=jnp.float32) / d)
    ang = pos.astype(jnp.float32)[:, None] * inv[None, :]
    cos = jnp.cos(ang)[:, None, :]
    sin = jnp.sin(ang)[:, None, :]
    xf = x.astype(jnp.float32)
    x1, x2 = xf[..., : d // 2], xf[..., d // 2:]
    return jnp.concatenate([x1 * cos - x2 * sin, x2 * cos + x1 * sin], axis=-1).astype(x.dtype)


def swiglu(x, w_gate, w_up, w_down):
    return (jax.nn.silu(x @ w_gate) * (x @ w_up)) @ w_down


def masked_softmax(s, mask):
    p = jax.nn.softmax(jnp.where(mask, s.astype(jnp.float32), NEG_BIG), axis=-1)
    return jnp.where(mask, p, 0.0)


def sweep(fn, n_blocks):
    out = jnp.moveaxis(lax.map(fn, jnp.arange(n_blocks)), 0, 1)
    return out.reshape((out.shape[0], -1) + out.shape[3:])


def cmp_to_slc_matrix(n_cmp, n_slc):
    c0 = np.arange(n_cmp) * CMP_STRIDE
    s0 = np.arange(n_slc) * SLC_BLOCK
    m = (c0[:, None] < s0[None, :] + SLC_BLOCK) & (c0[:, None] + CMP_BLOCK > s0[None, :])
    return jnp.asarray(m.astype(np.float32))


def nsa_attention(q, k_cmp, v_cmp, k_slc, v_slc, k_win, v_win, gates, pos_k, pos_v, phi_k1, phi_k2, phi_v1, phi_v2):
    B, S = q.shape[0], q.shape[1]
    scale = HEAD_DIM ** -0.5
    t = jnp.arange(S)
    n_cmp = (S - CMP_BLOCK) // CMP_STRIDE + 1
    starts = jnp.arange(n_cmp) * CMP_STRIDE
    idx = starts[:, None] + jnp.arange(CMP_BLOCK)[None, :]

    def compress(kv, pe, w1, w2):
        blk = kv[:, idx] + pe[:, None, :]
        blk = jnp.transpose(blk, (0, 1, 3, 2, 4)).reshape(B, n_cmp, NSA_KV_HEADS, CMP_BLOCK * HEAD_DIM)
        return jax.nn.gelu(blk @ w1) @ w2

    block_end = starts + CMP_BLOCK - 1
    kc = rope(compress(k_cmp, pos_k, phi_k1, phi_k2), block_end)
    vc = compress(v_cmp, pos_v, phi_v1, phi_v2)
    mask_c = block_end[None, :] <= t[:, None]
    s_c = jnp.einsum('bthgd,bchd->bhgtc', q, kc) * scale
    p_c = masked_softmax(s_c, mask_c)
    o_cmp = jnp.einsum('bhgtc,bchd->bthgd', p_c.astype(vc.dtype), vc)
    n_slc = S // SLC_BLOCK
    top = min(SLC_TOPK, n_slc)
    imp = jnp.sum(p_c, axis=2) @ cmp_to_slc_matrix(n_cmp, n_slc)
    j = jnp.arange(n_slc)[None, :]
    blk_t = (t // SLC_BLOCK)[:, None]
    forced = (j == 0) | (j == blk_t) | (j == blk_t - 1)
    valid = j * SLC_BLOCK <= t[:, None]
    score = jnp.where(forced, 1e9, jnp.where(valid, imp, -1.0))
    _, sel = lax.top_k(score, top)
    sel = jnp.transpose(sel, (0, 2, 1, 3))
    k_blk = jnp.transpose(k_slc.reshape(B, n_slc, SLC_BLOCK, NSA_KV_HEADS, HEAD_DIM), (0, 3, 1, 2, 4))
    v_blk = jnp.transpose(v_slc.reshape(B, n_slc, SLC_BLOCK, NSA_KV_HEADS, HEAD_DIM), (0, 3, 1, 2, 4))
    b_idx = jnp.arange(B)[:, None, None, None]
    h_idx = jnp.arange(NSA_KV_HEADS)[None, None, :, None]

    def slc_block(i):
        t0 = i * SLC_QUERY_BLOCK
        qb = lax.dynamic_slice_in_dim(q, t0, SLC_QUERY_BLOCK, axis=1)
        sb = lax.dynamic_slice_in_dim(sel, t0, SLC_QUERY_BLOCK, axis=1)
        kg = k_blk[b_idx, h_idx, sb]
        vg = v_blk[b_idx, h_idx, sb].reshape(B, SLC_QUERY_BLOCK, NSA_KV_HEADS, top * SLC_BLOCK, HEAD_DIM)
        key_pos = sb[..., None] * SLC_BLOCK + jnp.arange(SLC_BLOCK)
        tq = t0 + jnp.arange(SLC_QUERY_BLOCK)
        mask = (key_pos <= tq[None, :, None, None, None]).reshape(B, SLC_QUERY_BLOCK, NSA_KV_HEADS, 1, top * SLC_BLOCK)
        s = jnp.einsum('bqhgd,bqhkld->bqhgkl', qb, kg) * scale
        p = masked_softmax(s.reshape(B, SLC_QUERY_BLOCK, NSA_KV_HEADS, NSA_GROUP, top * SLC_BLOCK), mask)
        return jnp.einsum('bqhgn,bqhnd->bqhgd', p.astype(vg.dtype), vg)

    o_slc = sweep(slc_block, S // SLC_QUERY_BLOCK)
    k_pad = jnp.pad(k_win, ((0, 0), (WINDOW, 0), (0, 0), (0, 0)))
    v_pad = jnp.pad(v_win, ((0, 0), (WINDOW, 0), (0, 0), (0, 0)))

    def win_block(i):
        t0 = i * QUERY_BLOCK
        qb = lax.dynamic_slice_in_dim(q, t0, QUERY_BLOCK, axis=1)
        kb = lax.dynamic_slice_in_dim(k_pad, t0, WINDOW + QUERY_BLOCK, axis=1)
        vb = lax.dynamic_slice_in_dim(v_pad, t0, WINDOW + QUERY_BLOCK, axis=1)
        key_pos = t0 - WINDOW + jnp.arange(WINDOW + QUERY_BLOCK)
        tq = (t0 + jnp.arange(QUERY_BLOCK))[:, None]
        mask = (key_pos[None, :] <= tq) & (key_pos[None, :] > tq - WINDOW) & (key_pos[None, :] >= 0)
        s = jnp.einsum('bqhgd,bkhd->bhgqk', qb, kb) * scale
        p = masked_softmax(s, mask)
        return jnp.einsum('bhgqk,bkhd->bqhgd', p.astype(vb.dtype), vb)

    o_win = sweep(win_block, S // QUERY_BLOCK)
    out = gates[..., 0:1] * o_cmp + gates[..., 1:2] * o_slc + gates[..., 2:3] * o_win
    return out.reshape(B, S, NSA_WIDTH)


def diff_attention(q, k, v, lam_params, subln_g, lambda_init):
    B, S = q.shape[0], q.shape[1]
    lp = lam_params.astype(jnp.float32)
    lam = jnp.exp(jnp.sum(lp[0] * lp[1])) - jnp.exp(jnp.sum(lp[2] * lp[3])) + lambda_init
    scale = DIFF_QK_DIM ** -0.5
    kpos = jnp.arange(S)

    def block(i):
        t0 = i * QUERY_BLOCK
        qb = lax.dynamic_slice_in_dim(q, t0, QUERY_BLOCK, axis=1)
        mask = kpos[None, :] <= (t0 + jnp.arange(QUERY_BLOCK))[:, None]
        s = jnp.einsum('bqhcd,bkhcd->bhcqk', qb, k) * scale
        p = masked_softmax(s, mask)
        a = p[:, :, 0] - lam * p[:, :, 1]
        return jnp.einsum('bhqk,bkhd->bqhd', a.astype(v.dtype), v)

    o = sweep(block, S // QUERY_BLOCK).astype(jnp.float32)
    o = o * lax.rsqrt(jnp.mean(jnp.square(o), axis=-1, keepdims=True) + LN_EPS) * subln_g.astype(jnp.float32)
    o = o * (1.0 - lambda_init)
    return o.astype(v.dtype).reshape(B, S, DIFF_WIDTH)


def forgetting_attention(q, k, v, f_logit):
    B, S = q.shape[0], q.shape[1]
    c = jnp.cumsum(jax.nn.log_sigmoid(f_logit.astype(jnp.float32)), axis=1)
    cT = jnp.transpose(c, (0, 2, 1))
    scale = HEAD_DIM ** -0.5
    kpos = jnp.arange(S)

    def block(i):
        t0 = i * QUERY_BLOCK
        qb = lax.dynamic_slice_in_dim(q, t0, QUERY_BLOCK, axis=1)
        cq = lax.dynamic_slice_in_dim(cT, t0, QUERY_BLOCK, axis=2)
        mask = kpos[None, :] <= (t0 + jnp.arange(QUERY_BLOCK))[:, None]
        s = jnp.einsum('bqhd,bkhd->bhqk', qb, k).astype(jnp.float32) * scale + (cq[..., :, None] - cT[..., None, :])
        p = masked_softmax(s, mask)
        return jnp.einsum('bhqk,bkhd->bqhd', p.astype(v.dtype), v)

    return sweep(block, S // QUERY_BLOCK).reshape(B, S, FOX_WIDTH)


def hybrid_mixer(x, w_in, fox_b_f, nsa_pos_k, nsa_pos_v, nsa_phi_k1, nsa_phi_k2, nsa_phi_v1, nsa_phi_v2,
                 diff_lambda, diff_subln_g, w_out, lambda_init):
    B, S, _ = x.shape
    pos = jnp.arange(S, dtype=jnp.int32)
    proj = x @ w_in
    offsets = np.cumsum(IN_SPLITS)[:-1].tolist()
    nsa_q, nsa_kv, nsa_g, diff_q, diff_k, diff_v, fox_q, fox_k, fox_v, fox_f = jnp.split(proj, offsets, axis=-1)
    q = rope(nsa_q.reshape(B, S, NSA_HEADS, HEAD_DIM), pos).reshape(B, S, NSA_KV_HEADS, NSA_GROUP, HEAD_DIM)
    kv = nsa_kv.reshape(B, S, 6, NSA_KV_HEADS, HEAD_DIM)
    gates = jax.nn.sigmoid(nsa_g.reshape(B, S, NSA_KV_HEADS, NSA_GROUP, NSA_BRANCHES))
    o_nsa = nsa_attention(q, kv[:, :, 0], kv[:, :, 1], rope(kv[:, :, 2], pos), kv[:, :, 3],
                          rope(kv[:, :, 4], pos), kv[:, :, 5], gates,
                          nsa_pos_k, nsa_pos_v, nsa_phi_k1, nsa_phi_k2, nsa_phi_v1, nsa_phi_v2)
    dq = rope(diff_q.reshape(B, S, 2 * DIFF_HEADS, DIFF_QK_DIM), pos).reshape(B, S, DIFF_HEADS, 2, DIFF_QK_DIM)
    dk = rope(diff_k.reshape(B, S, 2 * DIFF_HEADS, DIFF_QK_DIM), pos).reshape(B, S, DIFF_HEADS, 2, DIFF_QK_DIM)
    o_diff = diff_attention(dq, dk, diff_v.reshape(B, S, DIFF_HEADS, DIFF_V_DIM), diff_lambda, diff_subln_g, lambda_init)
    o_fox = forgetting_attention(fox_q.reshape(B, S, FOX_HEADS, HEAD_DIM), fox_k.reshape(B, S, FOX_HEADS, HEAD_DIM),
                                 fox_v.reshape(B, S, FOX_HEADS, HEAD_DIM), fox_f + fox_b_f)
    return jnp.concatenate([o_nsa, o_diff, o_fox], axis=-1) @ w_out


def setup_inputs(seed: int = 0) -> dict:
    key = jax.random.key(seed)
    ks = iter(jax.random.split(key, 32))
    L = DEPTH

    def nrm(shape, scale):
        return jax.random.normal(next(ks), shape, jnp.float32) * scale

    return {
        "x": nrm((BATCH, SEQ, D_MODEL), 1.0),
        "p": nrm((DEPTH, BATCH, SEQ, PLE_DIM), 1.0),
        "ln_g": 1.0 + nrm((L, 3, D_MODEL), 0.01),
        "ln_b": nrm((L, 3, D_MODEL), 0.01),
        "ffn1_w_gate": nrm((L, D_MODEL, D_FF), D_MODEL ** -0.5),
        "ffn1_w_up": nrm((L, D_MODEL, D_FF), D_MODEL ** -0.5),
        "ffn1_w_down": nrm((L, D_FF, D_MODEL), DEEPNORM_BETA * D_FF ** -0.5),
        "ffn2_w_gate": nrm((L, D_MODEL, D_FF), D_MODEL ** -0.5),
        "ffn2_w_up": nrm((L, D_MODEL, D_FF), D_MODEL ** -0.5),
        "ffn2_w_down": nrm((L, D_FF, D_MODEL), DEEPNORM_BETA * D_FF ** -0.5),
        "w_in": nrm((L, D_MODEL, IN_COLS), D_MODEL ** -0.5),
        "fox_b_f": 2.0 + nrm((L, FOX_HEADS), 0.1),
        "nsa_pos_k": nrm((L, CMP_BLOCK, HEAD_DIM), 0.02),
        "nsa_pos_v": nrm((L, CMP_BLOCK, HEAD_DIM), 0.02),
        "nsa_phi_k1": nrm((L, CMP_BLOCK * HEAD_DIM, CMP_HIDDEN), (CMP_BLOCK * HEAD_DIM) ** -0.5),
        "nsa_phi_k2": nrm((L, CMP_HIDDEN, HEAD_DIM), CMP_HIDDEN ** -0.5),
        "nsa_phi_v1": nrm((L, CMP_BLOCK * HEAD_DIM, CMP_HIDDEN), (CMP_BLOCK * HEAD_DIM) ** -0.5),
        "nsa_phi_v2": nrm((L, CMP_HIDDEN, HEAD_DIM), CMP_HIDDEN ** -0.5),
        "diff_lambda": nrm((L, 4, DIFF_QK_DIM), 0.1),
        "diff_subln_g": 1.0 + nrm((L, DIFF_V_DIM), 0.01),
        "w_out": nrm((L, MIX_WIDTH, D_MODEL), DEEPNORM_BETA * MIX_WIDTH ** -0.5),
        "ple_w_gate": nrm((L, D_MODEL, D_MODEL), D_MODEL ** -0.5),
        "ple_b_gate": nrm((L, D_MODEL), 0.01),
        "ple_w_proj": nrm((L, PLE_DIM, D_MODEL), PLE_DIM ** -0.5),
    }


def reference(x, p, ln_g, ln_b, ffn1_w_gate, ffn1_w_up, ffn1_w_down, ffn2_w_gate, ffn2_w_up, ffn2_w_down,
              w_in, fox_b_f, nsa_pos_k, nsa_pos_v, nsa_phi_k1, nsa_phi_k2, nsa_phi_v1, nsa_phi_v2,
              diff_lambda, diff_subln_g, w_out, ple_w_gate, ple_b_gate, ple_w_proj):
    for i in range(DEPTH):
        lambda_init = 0.8 - 0.6 * math.exp(-0.3 * i)
        h = 0.5 * swiglu(x, ffn1_w_gate[i], ffn1_w_up[i], ffn1_w_down[i])
        x = layer_norm(DEEPNORM_ALPHA * x + h, ln_g[i, 0], ln_b[i, 0])
        h = hybrid_mixer(x, w_in[i], fox_b_f[i], nsa_pos_k[i], nsa_pos_v[i], nsa_phi_k1[i], nsa_phi_k2[i],
                         nsa_phi_v1[i], nsa_phi_v2[i], diff_lambda[i], diff_subln_g[i], w_out[i], lambda_init)
        x = layer_norm(DEEPNORM_ALPHA * x + h, ln_g[i, 1], ln_b[i, 1])
        h = 0.5 * swiglu(x, ffn2_w_gate[i], ffn2_w_up[i], ffn2_w_down[i])
        x = layer_norm(DEEPNORM_ALPHA * x + h, ln_g[i, 2], ln_b[i, 2])
        x = x + jax.nn.sigmoid(x @ ple_w_gate[i] + ple_b_gate[i]) * (p[i] @ ple_w_proj[i])
    return x
```

```python
import math
import contextlib
import numpy as np
import concourse.bass as bass
import concourse.mybir as mybir
from concourse.bass_utils import run_bass_kernel_spmd

F32 = mybir.dt.float32
BF16 = mybir.dt.bfloat16
AF = mybir.ActivationFunctionType
ALU = mybir.AluOpType
AX = mybir.AxisListType

D = 1024
S = 2048
KC = 8
NT = 4
NB = 16
DFF = 2752
NF = 22
INC = 2964
PLE = 256
DEPTH = 2
ALPHA = (2.0 * DEPTH) ** 0.25
EPS = 1e-5
SLOT = 3072
NSLOT = 6
ROLL = 10 ** 9
NCORES = 8
SEQ_PER_CORE = 4

C_NQ = 0
C_KCMP, C_VCMP, C_KSLC, C_VSLC, C_KWIN, C_VWIN = 256, 320, 384, 448, 512, 576
C_NG = 640
C_DQ, C_DK, C_DV = 652, 908, 1164
C_FQ, C_FK, C_FV, C_FF = 1420, 1932, 2444, 2956


class Buf:
    def __init__(self, ap, name=""):
        self.ap = ap
        self.w = {}
        self.r = {}
        self.name = name
        self.dead = False
        self.rng = None
        self.excl = False

    def __getitem__(self, idx):
        return self.ap[idx]


class Eng:
    def __init__(self, h, name):
        self.h = h
        self.name = name
        self.sem = None
        self.cnt = 0
        self.seen = {}
        self.nsem = 0


class KB:
    def __init__(self, nc, es):
        self.nc = nc
        self.es = es
        self.pe = Eng(nc.tensor, "pe")
        self.act = Eng(nc.scalar, "act")
        self.dve = Eng(nc.vector, "dve")
        self.pool = Eng(nc.gpsimd, "pool")
        self.sp = Eng(nc.sync, "sp")
        for e in (self.pe, self.act, self.dve, self.pool, self.sp):
            self._newsem(e)
        self.dq = {}
        self.out_toks = []

    def _newsem(self, e):
        e.sem = self.es.enter_context(self.nc.semaphore(f"s_{e.name}_{e.nsem}"))
        e.nsem += 1
        e.cnt = 0

    def _wait(self, e, tok):
        sem, val, owner = tok
        if owner is e and e.name == "pe":
            return
        k = id(sem)
        if e.seen.get(k, 0) >= val:
            return
        e.h.wait_ge(sem, val)
        e.seen[k] = val

    def _deps(self, e, reads, writes):
        need = {}

        def add(t):
            k = id(t[0])
            if k not in need or need[k][1] < t[1]:
                need[k] = t

        for b in reads:
            assert not b.dead, b.name
            for t in b.w.values():
                add(t)
            if b.excl:
                for t in b.r.values():
                    add(t)
        for b in writes:
            assert not b.dead, b.name
            for t in b.w.values():
                add(t)
            for t in b.r.values():
                add(t)
        for t in need.values():
            self._wait(e, t)

    def _upd(self, tok, reads, writes, is_dma=False):
        k = id(tok[0])
        for b in reads:
            old = b.r.get(k)
            if old is None or old[1] < tok[1]:
                b.r[k] = tok
        for b in writes:
            if is_dma:
                b.w[k] = tok
            else:
                b.w = {k: tok}
            b.r = {}

    def op(self, e, fn, reads=(), writes=()):
        self._deps(e, reads, writes)
        if e.cnt >= ROLL:
            self._newsem(e)
        ins = fn(e.h)
        e.cnt += 1
        ins.then_inc(e.sem, 1)
        tok = (e.sem, e.cnt, e)
        self._upd(tok, reads, writes)
        return tok

    def dma(self, e, out_ap, in_ap, reads=(), writes=()):
        self._deps(e, reads, writes)
        q = self.dq.setdefault(e.name, {"sems": [], "i": 0})
        K = 4 if e.name == "pool" else 8
        i = q["i"]
        q["i"] += 1
        if len(q["sems"]) < K:
            q["sems"].append([self.es.enter_context(self.nc.semaphore(f"d_{e.name}_{len(q['sems'])}")), 0])
        slot = q["sems"][i % K]
        sem, uses = slot
        if uses > 0:
            self._wait(e, (sem, 16 * uses, None))
        ins = e.h.dma_start(out=out_ap, in_=in_ap, allow_slow_non_contiguous=True)
        ins.then_inc(sem, 16)
        slot[1] = uses + 1
        tok = (sem, 16 * (uses + 1), None)
        self._upd(tok, reads, writes, is_dma=True)
        return tok

    def wait_tok(self, e, tok):
        self._wait(e, tok)


class Region:
    def __init__(self, kb, name, nbytes):
        self.kb = kb
        self.t = kb.es.enter_context(kb.nc.sbuf_tensor(name, [128, nbytes // 2], BF16))
        self.nbytes = nbytes
        self.live = []
        self.name = name

    def view(self, name, off, shape, dtype, parts=128):
        esz = 4 if dtype == F32 else 2
        n = int(np.prod(shape))
        nb = n * esz
        assert off % 4 == 0 and off + nb <= self.nbytes, (name, off, nb, self.nbytes)
        ap = self.t[0:parts, off // 2:(off + nb) // 2]
        if dtype == F32:
            ap = ap.bitcast(F32)
        if len(shape) == 2:
            ap = ap.rearrange("p (a b) -> p a b", b=shape[1])
        elif len(shape) == 3:
            ap = ap.rearrange("p (a b c) -> p a b c", b=shape[1], c=shape[2])
        b = Buf(ap, name)
        b.rng = (off, off + nb)
        keep = []
        for o in self.live:
            if o.rng[0] < b.rng[1] and b.rng[0] < o.rng[1]:
                o.dead = True
                for k, t in list(o.w.items()) + list(o.r.items()):
                    if k not in b.r or b.r[k][1] < t[1]:
                        b.r[k] = t
                if not (b.rng[0] <= o.rng[0] and o.rng[1] <= b.rng[1]):
                    keep.append(o)
            else:
                keep.append(o)
        keep.append(b)
        self.live = keep
        return b


class Ring:
    def __init__(self, kb, nslot):
        self.kb = kb
        self.nslot = nslot
        self.t = kb.es.enter_context(kb.nc.sbuf_tensor("ring", [128, nslot * SLOT], BF16))
        self.slots = [Buf(self.t[:, i * SLOT:(i + 1) * SLOT], f"slot{i}") for i in range(nslot)]
        self.owner = [None] * nslot
        self.chunks = []
        self.issued = 0
        self.cursor = 0

    def add(self, nsl, dmas):
        self.chunks.append({"n": nsl, "dmas": dmas, "start": None, "done": False})
        return len(self.chunks) - 1

    def _try_issue(self):
        c = self.chunks[self.issued]
        cur = self.cursor
        if cur + c["n"] > self.nslot:
            cur = 0
        for s in range(cur, cur + c["n"]):
            o = self.owner[s]
            if o is not None and not self.chunks[o]["done"]:
                return False
        c["start"] = cur
        bufs = self.slots[cur:cur + c["n"]]
        for (dst_fn, src) in c["dmas"]:
            dst = dst_fn(self.t, cur * SLOT)
            self.kb.dma(self.kb.pool, dst, src, writes=bufs)
        for s in range(cur, cur + c["n"]):
            self.owner[s] = self.issued
        self.cursor = cur + c["n"]
        self.issued += 1
        return True

    def acquire(self, i):
        while self.issued < len(self.chunks):
            if not self._try_issue():
                break
        c = self.chunks[i]
        assert c["start"] is not None, f"ring too small for chunk {i}"
        st = c["start"]
        return self.t, st * SLOT, self.slots[st:st + c["n"]]

    def release(self, i):
        self.chunks[i]["done"] = True


def mkap(base, off, dims):
    p = base.ap[0]
    return bass.AP(tensor=base.tensor, offset=base.offset + off, ap=[[p[0], p[1]]] + [[s, c] for s, c in dims])


def make_consts():
    c = {}
    c["ident"] = np.eye(128, dtype=np.float32)
    k = np.arange(128)[:, None]
    q = np.arange(128)[None, :]
    c["tri"] = np.where(k <= q, 0.0, -30000.0).astype(np.float32)
    c["anti"] = np.where(k > q, 0.0, -30000.0).astype(np.float32)
    c["identb"] = np.eye(128, dtype=np.float32)
    pos = np.arange(S, dtype=np.float32)

    def rope_tab(d):
        half = d // 2
        inv = (10000.0 ** (-np.arange(0, d, 2, dtype=np.float32) / d)).astype(np.float32)
        ang = pos[None, :] * inv[:, None]
        r = np.arange(128) % half
        return np.cos(ang)[r].astype(np.float32), np.sin(ang)[r].astype(np.float32)

    c["cos64"], c["sin64"] = rope_tab(64)
    c["cos32"], c["sin32"] = rope_tab(32)

    def rot_T(d):
        half = d // 2
        R = np.zeros((128, 128), np.float32)
        for m in range(128):
            g, i = divmod(m, d)
            if i < half:
                R[m, g * d + i + half] = -1.0
            else:
                R[m, g * d + i - half] = 1.0
        return np.ascontiguousarray(R.T)

    c["rt64"] = rot_T(64)
    c["rt32"] = rot_T(32)
    ncmp = 127
    cm = np.zeros((128, S), np.float32)
    cm[:ncmp] = ((np.arange(ncmp) * 16 + 31)[:, None] <= np.arange(S)[None, :]).astype(np.float32)
    c["cmpmask"] = cm
    c0 = np.arange(ncmp) * 16
    s0 = np.arange(32) * 64
    m = ((c0[:, None] < s0[None, :] + 64) & (c0[:, None] + 32 > s0[None, :])).astype(np.float32)
    mc = np.zeros((128, 32), np.float32)
    mc[:ncmp] = m
    c["mcs"] = mc
    E = np.zeros((128, S), np.float32)
    E[np.arange(S) // 64, np.arange(S)] = 1.0
    c["emat"] = E
    oh = np.zeros((128, 12, 64), np.float32)
    for r in range(12):
        oh[r, r, :] = 1.0
    c["oh"] = oh.reshape(128, 768)
    keep = np.zeros((128, 16, 32), np.float32)
    add = np.zeros((128, 16, 32), np.float32)
    for b in range(16):
        t = b * 128 + np.arange(128)
        j = np.arange(32)[None, :]
        blk = (t // 64)[:, None]
        forced0 = (j == 0)
        forced1 = (j == blk)
        forced2 = (j == blk - 1)
        forced = forced0 | forced1 | forced2
        valid = (j * 64 <= t[:, None])
        keep[:, b, :] = (valid & ~forced).astype(np.float32)
        a = np.where(valid, 0.0, -1.0)
        a = np.where(forced2, 1e9, a)
        a = np.where(forced1, 2e9, a)
        a = np.where(forced0 & np.ones_like(forced1), 3e9, a)
        add[:, b, :] = a
    c["keep"] = keep.reshape(128, 512)
    c["addt"] = add.reshape(128, 512)
    return c


CONST_SHAPES = {"identb": (128, 128), "ident": (128, 128), "tri": (128, 128), "anti": (128, 128), "cos64": (128, S), "sin64": (128, S),
                "cos32": (128, S), "sin32": (128, S), "rt64": (128, 128), "rt32": (128, 128), "cmpmask": (128, S),
                "mcs": (128, 32), "emat": (128, S), "oh": (128, 768), "keep": (128, 512), "addt": (128, 512)}

WEIGHT_SHAPES = {
    "ln_g": (DEPTH, 3, D), "ln_b": (DEPTH, 3, D),
    "ffn1_w_gate": (DEPTH, D, DFF), "ffn1_w_up": (DEPTH, D, DFF), "ffn1_w_down": (DEPTH, DFF, D),
    "ffn2_w_gate": (DEPTH, D, DFF), "ffn2_w_up": (DEPTH, D, DFF), "ffn2_w_down": (DEPTH, DFF, D),
    "w_in": (DEPTH, D, INC), "fox_b_f": (DEPTH, 8), "nsa_pos_k": (DEPTH, 32, 64), "nsa_pos_v": (DEPTH, 32, 64),
    "nsa_phi_k1": (DEPTH, 2048, 256), "nsa_phi_k2": (DEPTH, 256, 64),
    "nsa_phi_v1": (DEPTH, 2048, 256), "nsa_phi_v2": (DEPTH, 256, 64),
    "diff_lambda": (DEPTH, 4, 32), "diff_subln_g": (DEPTH, 64), "w_out": (DEPTH, D, D),
    "ple_w_gate": (DEPTH, D, D), "ple_b_gate": (DEPTH, D), "ple_w_proj": (DEPTH, PLE, D),
}


def build(nseq=SEQ_PER_CORE, depth=DEPTH, stop_after=None, dbg=()):
    nc = bass.Bass("TRN2", target_bir_lowering=False)
    es = contextlib.ExitStack()
    kb = KB(nc, es)
    pe, act, dve, pool, sp = kb.pe, kb.act, kb.dve, kb.pool, kb.sp

    x_d = nc.dram_tensor("x", [nseq, S, D], F32, kind="ExternalInput").ap()
    p_d = nc.dram_tensor("p", [DEPTH, nseq, S, PLE], F32, kind="ExternalInput").ap()
    W = {n: nc.dram_tensor(n, list(s), F32, kind="ExternalInput").ap() for n, s in WEIGHT_SHAPES.items()}
    Cd = {n: nc.dram_tensor("c_" + n, list(s), F32, kind="ExternalInput").ap() for n, s in CONST_SHAPES.items()}
    out_d = nc.dram_tensor("out", [nseq, S, D], F32, kind="ExternalOutput").ap()
    xsp_d = nc.dram_tensor("xsp", [128, KC * S], F32, kind="Internal").ap()
    dbg_d = {}
    if dbg:
        dbg_d["cat"] = nc.dram_tensor("dbg_cat", [128, KC * S], F32, kind="ExternalOutput").ap()
        dbg_d["qk"] = nc.dram_tensor("dbg_qk", [128, 2 * S], F32, kind="ExternalOutput").ap()

    def sb(name, shape, dt):
        return Buf(es.enter_context(nc.sbuf_tensor(name, list(shape), dt))[:], name)

    banks = [Buf(es.enter_context(nc.psum_tensor(f"ps{i}", [128, 512], F32))[:], f"ps{i}") for i in range(8)]
    for b_ in banks:
        b_.excl = True
    rot = {"A": 0, "B": 0, "P": 0, "T": 0, "H": 0, "R": 0}

    def rotA():
        rot["A"] = (rot["A"] + 1) % 4
        return banks[rot["A"]]

    def rotB():
        rot["B"] = (rot["B"] + 1) % 3
        return banks[4 + rot["B"]]

    bank_imp = banks[7]
    attn = {"on": False, "imp": False}
    rot["S"] = 0
    rot["M"] = 0

    def rotS():
        rot["S"] = (rot["S"] + 1) % 3
        return banks[rot["S"]]

    def rotX():
        if not attn["on"]:
            return rotA()
        if attn["imp"]:
            return banks[3]
        rot["M"] ^= 1
        return banks[3] if rot["M"] else banks[7]

    big = Region(kb, "big", 65536)
    xb = sb("xb", [128, KC, S], BF16)
    ring = Ring(kb, NSLOT)
    vreg = Region(kb, "vreg", 16384)
    ptiles = [sb(f"pt{i}", [128, 512], BF16) for i in range(4)]
    tmps = [sb(f"tf{i}", [128, 512], F32) for i in range(6)]
    htiles = [sb(f"ht{i}", [128, 512], BF16) for i in range(4)]
    rtmps = [sb(f"rt{i}", [128, 512], F32) for i in range(2)]
    rot["R"] = 0

    def rtf():
        rot["R"] = (rot["R"] + 1) % 2
        return rtmps[rot["R"]]

    def nextP():
        rot["P"] = (rot["P"] + 1) % 4
        return ptiles[rot["P"]]

    def tf():
        rot["T"] = (rot["T"] + 1) % 6
        return tmps[rot["T"]]

    def nextH():
        rot["H"] = (rot["H"] + 1) % 4
        return htiles[rot["H"]]

    import os
    SKIP = os.environ.get("KSKIP", "").split(",")
    cF = {}
    for n in ("ident", "rt64", "rt32"):
        cF[n] = sb("k_" + n, CONST_SHAPES[n], F32)
        kb.dma(sp, cF[n][:], Cd[n], writes=[cF[n]])
    for n in (("identb", "tri", "anti", "cos64", "sin64", "cos32", "sin32", "cmpmask", "mcs", "emat", "oh", "keep", "addt") if "cb" not in SKIP else ()):
        cF[n] = sb("k_" + n, CONST_SHAPES[n], BF16)
        kb.dma(pool, cF[n][:], Cd[n], writes=[cF[n]])
    ones_b = sb("ones_b", [128, 128], BF16)
    kb.op(dve, lambda h: h.memset(ones_b[:], 1.0), writes=[ones_b])
    oD_b = sb("oD_b", [128, 128], BF16)
    kb.op(dve, lambda h: h.memset(oD_b[:], 1.0 / D), writes=[oD_b])
    o64_b = sb("o64_b", [128, 64], BF16)
    kb.op(dve, lambda h: h.memset(o64_b[:], 1.0 / 64.0), writes=[o64_b])
    eps_t = sb("eps_t", [128, 1], F32)
    kb.op(dve, lambda h: h.memset(eps_t[:], EPS), writes=[eps_t])
    zer_b = sb("zer_b", [128, 512], BF16)
    kb.op(dve, lambda h: h.memset(zer_b[:], 0.0), writes=[zer_b])

    lng = sb("lng", [128, DEPTH * 3 * KC], F32)
    lnb = sb("lnb", [128, DEPTH * 3 * KC], F32)
    lngs = sb("lngs", [128, DEPTH * 3 * KC], F32)
    lnbs = sb("lnbs", [128, DEPTH * 3 * KC], F32)
    pleb = sb("pleb", [128, DEPTH * KC], F32)
    for l in range(DEPTH if "ln" not in SKIP else 0):
        for i in range(3):
            o = (l * 3 + i) * KC
            kb.dma(sp, lng[:, o:o + KC], W["ln_g"][l, i].rearrange("(c p) -> p c", p=128), writes=[lng])
            kb.dma(sp, lnb[:, o:o + KC], W["ln_b"][l, i].rearrange("(c p) -> p c", p=128), writes=[lnb])
        kb.dma(sp, pleb[:, l * KC:(l + 1) * KC], W["ple_b_gate"][l].rearrange("(c p) -> p c", p=128), writes=[pleb])
    kb.op(dve, lambda h: h.tensor_scalar(out=lngs[:], in0=lng[:], scalar1=ALPHA, scalar2=None, op0=ALU.mult),
          reads=[lng], writes=[lngs])
    kb.op(dve, lambda h: h.tensor_scalar(out=lnbs[:], in0=lnb[:], scalar1=ALPHA, scalar2=None, op0=ALU.mult),
          reads=[lnb], writes=[lnbs])
    nbf = sb("nbf", [8, DEPTH], F32)
    if "nbf" not in SKIP:
        kb.dma(sp, nbf[:], W["fox_b_f"].rearrange("l h -> h l"), writes=[nbf])
    kb.op(dve, lambda h: h.tensor_scalar(out=nbf[:], in0=nbf[:], scalar1=-1.0, scalar2=None, op0=ALU.mult),
          reads=[nbf], writes=[nbf])
    peT = sb("peT", [64, DEPTH * 2 * 32], F32)
    for l in range(DEPTH if "pet" not in SKIP else 0):
        kb.dma(sp, peT[:, (l * 2) * 32:(l * 2 + 1) * 32], W["nsa_pos_k"][l].rearrange("a d -> d a"), writes=[peT])
        kb.dma(sp, peT[:, (l * 2 + 1) * 32:(l * 2 + 2) * 32], W["nsa_pos_v"][l].rearrange("a d -> d a"), writes=[peT])
    dlam = sb("dlam", [64, DEPTH * 128], F32)
    for l in range(DEPTH if "dlam" not in SKIP else 0):
        src = W["diff_lambda"][l].rearrange("a b -> (a b)")
        kb.dma(sp, dlam[:, l * 128:(l + 1) * 128], mkap(src, 0, [(1, 128)]) if False else
               bass.AP(tensor=src.tensor, offset=src.offset, ap=[[0, 64], [1, 128]]), writes=[dlam])
    dprod = sb("dprod", [64, DEPTH * 64], F32)
    nlam = sb("nlam", [64, DEPTH], F32)
    dsum = sb("dsum", [64, DEPTH * 2], F32)
    dgs = sb("dgs", [64, DEPTH], F32)
    if "dlam" not in SKIP:
        kb.dma(sp, dgs[:], W["diff_subln_g"].rearrange("l d -> d l"), writes=[dgs])
    for l in range(DEPTH if "dlam" not in SKIP else 0):
        linit = 0.8 - 0.6 * math.exp(-0.3 * l)
        for a in range(2):
            o = l * 128 + a * 64
            kb.op(dve, lambda h: h.tensor_tensor(out=dprod[:, l * 64 + a * 32:l * 64 + (a + 1) * 32],
                                                 in0=dlam[:, o:o + 32], in1=dlam[:, o + 32:o + 64], op=ALU.mult),
                  reads=[dlam], writes=[dprod])
            kb.op(dve, lambda h: h.reduce_sum(out=dsum[:, l * 2 + a:l * 2 + a + 1],
                                              in_=dprod[:, l * 64 + a * 32:l * 64 + (a + 1) * 32], axis=AX.X),
                  reads=[dprod], writes=[dsum])
        kb.op(act, lambda h: h.activation(out=dsum[:, l * 2:l * 2 + 2], in_=dsum[:, l * 2:l * 2 + 2], func=AF.Exp),
              reads=[dsum], writes=[dsum])
        kb.op(dve, lambda h: h.tensor_tensor(out=nlam[:, l:l + 1], in0=dsum[:, l * 2 + 1:l * 2 + 2],
                                             in1=dsum[:, l * 2:l * 2 + 1], op=ALU.subtract),
              reads=[dsum], writes=[nlam])
        kb.op(dve, lambda h: h.tensor_scalar(out=nlam[:, l:l + 1], in0=nlam[:, l:l + 1], scalar1=-linit, scalar2=None,
                                             op0=ALU.add), reads=[nlam], writes=[nlam])
        kb.op(dve, lambda h: h.tensor_scalar(out=dgs[:, l:l + 1], in0=dgs[:, l:l + 1], scalar1=1.0 - linit,
                                             scalar2=None, op0=ALU.mult), reads=[dgs], writes=[dgs])

    x32 = [None]

    def X32():
        return x32[0]

    def tile_sl(t):
        return slice(t * 512, (t + 1) * 512)

    def load_x(si):
        x32[0] = big.view("x32", 0, (KC, S), F32)
        X = x32[0]
        for t in range(NT):
            stage = vreg.view("xstage", 0, (4, 1024), F32)
            for i in range(4):
                kb.dma(sp, stage[:, i, :], x_d[si, (4 * t + i) * 128:(4 * t + i + 1) * 128, :], writes=[stage])
            for c in range(KC):
                pst = rotA()
                for i in range(4):
                    kb.op(pe, lambda h: h.transpose(pst[:, i * 128:(i + 1) * 128], stage[:, i, c * 128:(c + 1) * 128],
                                                    cF["ident"][:]), reads=[stage, cF["ident"]], writes=[pst])
                kb.op(act, lambda h: h.mul(X[:, c, tile_sl(t)], pst[:], ALPHA), reads=[pst], writes=[X])
                kb.op(act, lambda h: h.copy(xb[:, c, tile_sl(t)], pst[:]), reads=[pst], writes=[xb])

    def wview(w2d):
        return w2d.rearrange("(k p) c -> p k c", p=128)

    def dst3(n0, k, c):
        return lambda t, o: t[:, o + n0:o + n0 + k * c].rearrange("p (k c) -> p k c", c=c)

    def ffn_chunks(l, which):
        wg = wview(W[f"ffn{which}_w_gate"][l])
        wu = wview(W[f"ffn{which}_w_up"][l])
        wd = W[f"ffn{which}_w_down"][l]
        ids = []
        for f in range(NF):
            fw = 128 if f < NF - 1 else DFF - 128 * (NF - 1)
            dmas = [(dst3(0, KC, fw), wg[:, :, f * 128:f * 128 + fw]),
                    (dst3(1024, KC, fw), wu[:, :, f * 128:f * 128 + fw]),
                    ((lambda t, o, fw=fw: t[0:fw, o + 2048:o + 3072]), wd[f * 128:f * 128 + fw, :])]
            ids.append(ring.add(1, dmas))
        return ids

    def ffn(ids):
        X = X32()
        steps = [(g, t) for g in range(NF // 2) for t in range(NT)]
        acq = {}

        def gu_parts(g, t):
            fs = (2 * g, 2 * g + 1)
            if g not in acq:
                acq[g] = [ring.acquire(ids[f]) for f in fs]
            Hs = []
            parts = []
            for fi, f in enumerate(fs):
                fw = 128 if f < NF - 1 else DFF - 128 * (NF - 1)
                tt, o, sbufs = acq[g][fi]
                st = {}

                def mm(key, off, k0, k1, fw=fw, tt=tt, o=o, sbufs=sbufs, st=st):
                    if k0 == 0:
                        st[key] = rotA()
                    ps = st[key]
                    for k in range(k0, k1):
                        kb.op(pe, lambda h: h.matmul(ps[0:fw, :], lhsT=tt[:, o + off + k * fw:o + off + (k + 1) * fw],
                                                     rhs=xb[:, k, tile_sl(t)], start=(k == 0), stop=(k == KC - 1)),
                              reads=[xb] + sbufs, writes=[ps])

                def fin(fw=fw, tt=tt, o=o, sbufs=sbufs, st=st):
                    psg, psu = st["g"], st["u"]
                    sg = tf()
                    kb.op(act, lambda h: h.activation(out=sg[0:fw, :], in_=psg[0:fw, :], func=AF.Silu),
                          reads=[psg], writes=[sg])
                    Hb = nextH()
                    kb.op(dve, lambda h: h.tensor_tensor(out=Hb[0:fw, :], in0=sg[0:fw, :], in1=psu[0:fw, :], op=ALU.mult),
                          reads=[sg, psu], writes=[Hb])
                    Hs.append((Hb, fw, tt, o, sbufs))

                parts.append(lambda mm=mm: mm("g", 0, 0, 4))
                parts.append(lambda mm=mm: mm("g", 0, 4, 8))
                parts.append(lambda mm=mm: mm("u", 1024, 0, 4))
                parts.append(lambda mm=mm, fin=fin: (mm("u", 1024, 4, 8), fin()))
            return parts, Hs

        def down_m(t, Hs, m):
            psd = rotB()
            for i, (Hb, fw, tt, o, sbufs) in enumerate(Hs):
                kb.op(pe, lambda h: h.matmul(psd[:, :], lhsT=tt[0:fw, o + 2048 + m * 128:o + 2048 + (m + 1) * 128],
                                             rhs=Hb[0:fw, :], start=(i == 0), stop=(i == len(Hs) - 1)),
                      reads=[Hb] + sbufs, writes=[psd])
            kb.op(dve, lambda h: h.scalar_tensor_tensor(out=X[:, m, tile_sl(t)], in0=psd[:, :], scalar=0.5,
                                                        in1=X[:, m, tile_sl(t)], op0=ALU.mult, op1=ALU.add),
                  reads=[psd, X], writes=[X])

        parts, Hs = gu_parts(*steps[0])
        for p_ in parts:
            p_()
        for i, (g, t) in enumerate(steps):
            if i + 1 < len(steps):
                nparts, nHs = gu_parts(*steps[i + 1])
            else:
                nparts, nHs = [], []
            for m in range(KC):
                down_m(t, Hs, m)
                if nparts:
                    nparts.pop(0)()
            while nparts:
                nparts.pop(0)()
            if t == NT - 1:
                for f in (2 * g, 2 * g + 1):
                    ring.release(ids[f])
            Hs = nHs

    def ln_tile(Y, ysl, l, idx, scaled, xb_sl):
        yb = vreg.view("lnyb", 0, (KC, 512), BF16)
        ysq = vreg.view("lnysq", 8192, (KC, 512), BF16)
        kb.op(act, lambda h: h.activation(out=ysq[:], in_=Y[:, :, ysl], func=AF.Square), reads=[Y], writes=[ysq])
        kb.op(dve, lambda h: h.tensor_copy(out=yb[:], in_=Y[:, :, ysl]), reads=[Y], writes=[yb])
        ps1 = rotA()
        ps2 = rotA()
        for c in range(KC):
            kb.op(pe, lambda h: h.matmul(ps1[:, :], lhsT=oD_b[:], rhs=yb[:, c, :], start=(c == 0), stop=(c == KC - 1)),
                  reads=[yb, oD_b], writes=[ps1])
        for c in range(KC):
            kb.op(pe, lambda h: h.matmul(ps2[:, :], lhsT=oD_b[:], rhs=ysq[:, c, :], start=(c == 0), stop=(c == KC - 1)),
                  reads=[ysq, oD_b], writes=[ps2])
        msq = tf()
        kb.op(act, lambda h: h.activation(out=msq[:], in_=ps1[:], func=AF.Square), reads=[ps1], writes=[msq])
        var = tf()
        kb.op(dve, lambda h: h.tensor_tensor(out=var[:], in0=ps2[:], in1=msq[:], op=ALU.subtract), reads=[ps2, msq], writes=[var])
        kb.op(act, lambda h: h.activation(out=var[:], in_=var[:], func=AF.Ln, bias=eps_t[:, 0:1]), reads=[var, eps_t], writes=[var])
        kb.op(act, lambda h: h.activation(out=ps2[:], in_=var[:], func=AF.Exp, scale=-0.5), reads=[var], writes=[ps2])
        mb = mkap(ps1[:], 0, [(0, KC), (1, 512)])
        rb = mkap(ps2[:], 0, [(0, KC), (1, 512)])
        kb.op(dve, lambda h: h.tensor_tensor(out=Y[:, :, ysl], in0=Y[:, :, ysl], in1=mb, op=ALU.subtract),
              reads=[Y, ps1], writes=[Y])
        kb.op(dve, lambda h: h.tensor_tensor(out=Y[:, :, ysl], in0=Y[:, :, ysl], in1=rb, op=ALU.mult),
              reads=[Y, ps2], writes=[Y])
        o = (l * 3 + idx) * KC
        gs, bs = (lngs, lnbs) if scaled else (lng, lnb)
        for c in range(KC):
            kb.op(act, lambda h: h.activation(out=Y[:, c, ysl], in_=Y[:, c, ysl], func=AF.Identity,
                                              scale=gs[:, o + c:o + c + 1], bias=bs[:, o + c:o + c + 1]),
                  reads=[Y, gs, bs], writes=[Y])
        kb.op(dve, lambda h: h.tensor_scalar(out=xb[:, :, xb_sl], in0=Y[:, :, ysl], scalar1=(1.0 / ALPHA) if scaled else 1.0,
                                             scalar2=None, op0=ALU.mult), reads=[Y], writes=[xb])

    def layernorm(l, idx, scaled):
        for t in range(NT):
            ln_tile(X32(), tile_sl(t), l, idx, scaled, tile_sl(t))

    def ple(l, si, last, nxt=None):
        X = X32()
        wg = wview(W["ple_w_gate"][l])
        wp = W["ple_w_proj"][l].rearrange("(k p) c -> p k c", p=128)
        ids = []
        for m in range(KC):
            dmas = [(dst3(0, KC, 128), wg[:, :, m * 128:(m + 1) * 128]),
                    (dst3(1024, 2, 128), wp[:, :, m * 128:(m + 1) * 128])]
            ids.append(ring.add(1, dmas))
        if nxt is not None:
            MIXER["ffn1_ids"] = ffn_chunks(nxt, 1)
        pT = vreg.view("pT", 0, (2, S), BF16)
        for t in range(NT):
            stage = vreg.view(f"pstage", 8192, (4, PLE), F32)
            for i in range(4):
                kb.dma(sp, stage[:, i, :], p_d[l, si, (4 * t + i) * 128:(4 * t + i + 1) * 128, :], writes=[stage])
            for j in range(2):
                pst = rotA()
                for i in range(4):
                    kb.op(pe, lambda h: h.transpose(pst[:, i * 128:(i + 1) * 128], stage[:, i, j * 128:(j + 1) * 128],
                                                    cF["ident"][:]), reads=[stage, cF["ident"]], writes=[pst])
                kb.op(dve, lambda h: h.tensor_copy(out=pT[:, j, tile_sl(t)], in_=pst[:]), reads=[pst], writes=[pT])
        for m in range(KC):
            tt, o, sbufs = ring.acquire(ids[m])
            for t in range(NT):
                ps1 = rotA()
                ps2 = rotA()
                for k in range(KC):
                    kb.op(pe, lambda h: h.matmul(ps1[:, :], lhsT=tt[:, o + k * 128:o + (k + 1) * 128],
                                                 rhs=xb[:, k, tile_sl(t)], start=(k == 0), stop=(k == KC - 1)),
                          reads=[xb] + sbufs, writes=[ps1])
                for j in range(2):
                    kb.op(pe, lambda h: h.matmul(ps2[:, :], lhsT=tt[:, o + 1024 + j * 128:o + 1024 + (j + 1) * 128],
                                                 rhs=pT[:, j, tile_sl(t)], start=(j == 0), stop=(j == 1)),
                          reads=[pT] + sbufs, writes=[ps2])
                sg = tf()
                kb.op(act, lambda h: h.activation(out=sg[:], in_=ps1[:], func=AF.Sigmoid,
                                                  bias=pleb[:, l * KC + m:l * KC + m + 1]),
                      reads=[ps1, pleb], writes=[sg])
                kb.op(dve, lambda h: h.tensor_tensor(out=sg[:], in0=sg[:], in1=ps2[:], op=ALU.mult),
                      reads=[sg, ps2], writes=[sg])
                kb.op(dve, lambda h: h.tensor_tensor(out=X[:, m, tile_sl(t)], in0=X[:, m, tile_sl(t)], in1=sg[:],
                                                     op=ALU.add), reads=[sg, X], writes=[X])
            ring.release(ids[m])
        if not last:
            for c in range(KC):
                kb.op(act, lambda h: h.copy(xb[:, c, :], X[:, c, :]), reads=[X], writes=[xb])
                kb.op(dve, lambda h: h.tensor_scalar(out=X[:, c, :], in0=X[:, c, :], scalar1=ALPHA, scalar2=None,
                                                     op0=ALU.mult), reads=[X], writes=[X])

    def store_out(si, scale=1.0):
        X = X32()
        for b in range(NB):
            stage = vreg.view("ostage", (b % 2) * 4096, (1024,), F32)
            for half in range(2):
                pst = rotA()
                for cc in range(4):
                    c = half * 4 + cc
                    kb.op(pe, lambda h: h.transpose(pst[:, cc * 128:(cc + 1) * 128], X[:, c, b * 128:(b + 1) * 128],
                                                    cF["ident"][:]), reads=[X, cF["ident"]], writes=[pst])
                kb.op(act, lambda h: h.mul(stage[:, half * 512:(half + 1) * 512], pst[:], scale),
                      reads=[pst], writes=[stage])
            tok = kb.dma(sp, out_d[si, b * 128:(b + 1) * 128, :], stage[:], reads=[stage])
            kb.out_toks.append(tok)

    MIXER = {}
    xspB = Buf(None, "xsp")
    ones_f = sb("ones_f", [128, 8], F32)
    kb.op(dve, lambda h: h.memset(ones_f[:], 1.0), writes=[ones_f])
    csT = sb("csT", [128, 128], F32)
    def causal_struct(j):
        res = []
        for kt in range(4 * j + 4):
            if kt < 4 * j:
                res.append((kt, 0, 512, []))
            else:
                i = kt - 4 * j
                res.append((kt, 128 * i, 512, [(i, "tri")]))
        return res

    def window_struct(j):
        res = []
        for m in range(8):
            kt = 4 * j - 4 + m
            if kt < 0:
                continue
            lo, hi = max(0, m - 4), min(3, m)
            masks = []
            if m <= 3:
                masks.append((m, "anti"))
            if m >= 4:
                masks.append((m - 4, "tri"))
            res.append((kt, 128 * lo, 128 * (hi + 1), masks))
        return res

    LOOK = 2

    def attend(j, struct, QT, qbufs, KT_fn, kbufs, nk_fn, vaug_fn, vbufs, scale, bias_fn=None, bbufs=(), hook=None, bg=None, bgs=None, smask=None):
        acc = rotB()
        lst = struct(j)
        n = len(lst)
        pss_l = [None] * n

        def emit_s(idx):
            kt, c0, c1, masks = lst[idx]
            nk = nk_fn(kt)
            pss = rotS()
            pss_l[idx] = pss
            sm = smask(kt, j, c0, c1) if smask is not None else None
            kb.op(pe, lambda h: h.matmul(pss[0:nk, c0:c1], lhsT=KT_fn(kt), rhs=QT[:, j * 512 + c0:j * 512 + c1],
                                         start=True, stop=(len(masks) == 0 and sm is None)),
                  reads=list(qbufs) + list(kbufs), writes=[pss])
            if sm is not None:
                sm_l, sm_r, sm_bufs = sm
                kb.op(pe, lambda h: h.matmul(pss[0:nk, c0:c1], lhsT=sm_l, rhs=sm_r, start=False, stop=(len(masks) == 0)),
                      reads=list(sm_bufs), writes=[pss])
            for mi, (i, mname) in enumerate(masks):
                mk = cF[mname]
                kb.op(pe, lambda h: h.matmul(pss[0:nk, i * 128:(i + 1) * 128], lhsT=cF["identb"][:, 0:nk], rhs=mk[:, :],
                                             start=False, stop=(mi == len(masks) - 1)), reads=[mk, cF["identb"]], writes=[pss])

        for idx in range(min(LOOK, n)):
            emit_s(idx)
        for idx, (kt, c0, c1, masks) in enumerate(lst):
            nk = nk_fn(kt)
            pss = pss_l[idx]
            P = nextP()
            if bias_fn is not None:
                kb.op(act, lambda h: h.activation(out=P[0:nk, c0:c1], in_=pss[0:nk, c0:c1], func=AF.Exp, scale=scale,
                                                  bias=bias_fn(kt)), reads=[pss] + list(bbufs), writes=[P])
            else:
                kb.op(act, lambda h: h.activation(out=P[0:nk, c0:c1], in_=pss[0:nk, c0:c1], func=AF.Exp, scale=scale),
                      reads=[pss], writes=[P])
            if idx + LOOK < n:
                emit_s(idx + LOOK)
            if hook is not None:
                hook(kt, j, c0, c1, P, nk)
            kb.op(pe, lambda h: h.matmul(acc[:, c0:c1], lhsT=vaug_fn(kt), rhs=P[0:nk, c0:c1], start=(idx == 0),
                                         stop=(idx == n - 1), skip_group_check=True), reads=[P] + list(vbufs), writes=[acc])
            if bg:
                if bgs is None:
                    bg.pop(0)()
                else:
                    bgs["n"] += 1
                    if bgs["n"] % bgs["every"] == 0:
                        bg.pop(0)()
        return acc

    def recip_act(ap, bufs):
        kb.op(act, lambda h: h.activation(out=ap, in_=ap, func=AF.Ln), reads=list(bufs), writes=list(bufs))
        kb.op(act, lambda h: h.activation(out=ap, in_=ap, func=AF.Exp, scale=-1.0), reads=list(bufs), writes=list(bufs))

    def norm_out(acc, dst_ap, dst_bufs, mul_ap=None, mul_bufs=(), use_act=False):
        r = tf()
        if use_act:
            kb.op(act, lambda h: h.activation(out=r[64:128, :], in_=acc[0:64, :], func=AF.Ln), reads=[acc], writes=[r])
            kb.op(act, lambda h: h.activation(out=r[64:128, :], in_=r[64:128, :], func=AF.Exp, scale=-1.0), reads=[r], writes=[r])
        else:
            kb.op(dve, lambda h: h.reciprocal(out=r[64:128, :], in_=acc[0:64, :]), reads=[acc], writes=[r])
        if mul_ap is not None:
            kb.op(dve, lambda h: h.tensor_tensor(out=r[64:128, :], in0=r[64:128, :], in1=mul_ap, op=ALU.mult),
                  reads=[r] + list(mul_bufs), writes=[r])
        kb.op(dve, lambda h: h.tensor_tensor(out=dst_ap, in0=acc[64:128, :], in1=r[64:128, :], op=ALU.mult),
              reads=[acc, r], writes=list(dst_bufs))

    def proj_fm(tt, o, ncol, kstride, sbufs, t, rows):
        ps = rotX()
        for k in range(KC):
            kb.op(pe, lambda h: h.matmul(ps[0:rows, :], lhsT=tt[:, o + k * kstride:o + k * kstride + rows],
                                         rhs=xb[:, k, tile_sl(t)], start=(k == 0), stop=(k == KC - 1)),
                  reads=[xb] + sbufs, writes=[ps])
        return ps

    def rope_a(ps, rows):
        t1 = rtf()
        kb.op(act, lambda h: h.copy(t1[0:rows, :], ps[0:rows, :]), reads=[ps], writes=[t1])
        return t1

    def rope_b(t1, rows, t, dst_ap, dst_bufs, d, dsts=None):
        cosT, sinT, rt = (cF["cos64"], cF["sin64"], cF["rt64"]) if d == 64 else (cF["cos32"], cF["sin32"], cF["rt32"])
        p2 = rotX()
        kb.op(pe, lambda h: h.matmul(p2[0:rows, :], lhsT=rt[0:rows, 0:rows], rhs=t1[0:rows, :], start=True, stop=True),
              reads=[t1, rt], writes=[p2])
        t2 = rtf()
        kb.op(dve, lambda h: h.tensor_tensor(out=t2[0:rows, :], in0=p2[0:rows, :], in1=sinT[0:rows, tile_sl(t)], op=ALU.mult),
              reads=[p2, sinT], writes=[t2])
        kb.op(dve, lambda h: h.tensor_tensor(out=t1[0:rows, :], in0=t1[0:rows, :], in1=cosT[0:rows, tile_sl(t)], op=ALU.mult),
              reads=[t1, cosT], writes=[t1])
        if dsts is None:
            dsts = [(0, rows, dst_ap, dst_bufs)]
        for (r0, r1, dap, dbufs) in dsts:
            kb.op(dve, lambda h: h.tensor_tensor(out=dap, in0=t1[r0:r1, :], in1=t2[r0:r1, :], op=ALU.add),
                  reads=[t1, t2], writes=list(dbufs))

    def rope_evac(ps, rows, t, dst_ap, dst_bufs, d):
        t1 = rope_a(ps, rows)
        rope_b(t1, rows, t, dst_ap, dst_bufs, d)

    def vproj(col0, vaug):
        w = 256
        cid = ring.add(1, [(dst3(0, KC, w), win_v[0][:, :, col0:col0 + w])])
        tt, o, sbufs = ring.acquire(cid)
        for blk in range(NB):
            ps = rotA()
            for k in range(KC):
                kb.op(pe, lambda h: h.matmul(ps[:, 0:w], lhsT=xb[:, k, blk * 128:(blk + 1) * 128],
                                             rhs=tt[:, o + k * w:o + (k + 1) * w], start=(k == 0), stop=(k == KC - 1)),
                      reads=[xb] + sbufs, writes=[ps])
            kb.op(act, lambda h: h.copy(vaug[:, blk, :, 64:128], ps[:, 0:w].rearrange("p (h d) -> p h d", d=64)),
                  reads=[ps], writes=[vaug])
        ring.release(cid)

    win_v = [None]

    def vaug_ap(vaug, kt, hl):
        return vaug[:, kt, hl, :]

    def mixer(l, si):
        global_win = wview(W["w_in"][l])
        win_v[0] = global_win
        X = X32()
        xsp3 = xsp_d.rearrange("p (c s) -> p c s", s=S)
        for c in range(KC):
            kb.dma(sp, xsp3[:, c, :], X[:, c, :], reads=[X], writes=[xspB])
        concat = big.view("concat", 0, (KC, S), BF16)
        QO = 32768
        win = global_win

        spb = big.view("fsp", QO + 16384, (S,), F32)
        ncs8 = big.view("fncs8", QO + 24576, (S,), BF16)
        cid = ring.add(1, [(dst3(0, KC, 8), win[:, :, C_FF:C_FF + 8])])
        tt, o, sbufs = ring.acquire(cid)
        for t in range(NT):
            ps = proj_fm(tt, o, 8, 8, sbufs, t, 8)
            e1 = tf()
            kb.op(act, lambda h: h.activation(out=e1[0:8, :], in_=ps[0:8, :], func=AF.Exp, scale=-1.0, bias=nbf[:, l:l + 1]),
                  reads=[ps, nbf], writes=[e1])
            kb.op(act, lambda h: h.activation(out=spb[0:8, tile_sl(t)], in_=e1[0:8, :], func=AF.Ln, bias=1.0),
                  reads=[e1], writes=[spb])
        ring.release(cid)
        kb.op(dve, lambda h: h.tensor_tensor_scan(out=spb[0:8, :], data0=mkap(ones_f[0:8, 0:1], 0, [(0, S)]),
                                                  data1=spb[0:8, :], initial=0.0, op0=ALU.mult, op1=ALU.add),
              reads=[spb, ones_f], writes=[spb])
        kb.op(dve, lambda h: h.tensor_scalar(out=ncs8[0:8, :], in0=spb[0:8, :], scalar1=-8.0, scalar2=None, op0=ALU.mult),
              reads=[spb], writes=[ncs8])
        pst = rotA()
        for blk in range(NB):
            kb.op(pe, lambda h: h.transpose(pst[:, blk * 8:(blk + 1) * 8], spb[0:8, blk * 128:(blk + 1) * 128],
                                            cF["ident"][0:8, 0:8]), reads=[spb, cF["ident"]], writes=[pst])
        kb.op(dve, lambda h: h.tensor_copy(out=csT[:, 0:128], in_=pst[:, 0:128]), reads=[pst], writes=[csT])
        fbuf = [(big.view("fqa0", QO, (S,), BF16), big.view("fka0", QO + 4096, (S,), BF16)),
                (big.view("fqa1", QO + 8192, (S,), BF16), big.view("fka1", QO + 12288, (S,), BF16))]

        def fox_jobs(hh):
            qa, ka = fbuf[hh % 2]
            st = {}

            def j_acq():
                qk3 = lambda t_, o_: t_[:, o_:o_ + 1024].rearrange("p (k c) -> p k c", c=128)
                cid = ring.add(1, [((lambda t_, o_: qk3(t_, o_)[:, :, 0:64]), win[:, :, C_FQ + hh * 64:C_FQ + (hh + 1) * 64]),
                                   ((lambda t_, o_: qk3(t_, o_)[:, :, 64:128]), win[:, :, C_FK + hh * 64:C_FK + (hh + 1) * 64])])
                st["cid"] = cid
                st["acq"] = ring.acquire(cid)
                kb.dma(sp, qa[64:65, :], ncs8[hh:hh + 1, :], reads=[ncs8], writes=[qa])
                kb.op(dve, lambda h: h.memset(ka[64:65, :], 1.0), writes=[ka])

            def j_proj(t):
                def f():
                    tt, o, sbufs = st["acq"]
                    ps = proj_fm(tt, o, 128, 128, sbufs, t, 128)
                    kb.op(dve, lambda h: h.tensor_copy(out=qa[0:64, tile_sl(t)], in_=ps[0:64, :]), reads=[ps], writes=[qa])
                    kb.op(dve, lambda h: h.tensor_copy(out=ka[0:64, tile_sl(t)], in_=ps[64:128, :]), reads=[ps], writes=[ka])
                    if t == NT - 1:
                        ring.release(st["cid"])
                return f

            return [j_acq] + [j_proj(t) for t in range(NT)]

        for job in fox_jobs(0):
            job()
        attn["on"] = True
        for hh in range(8):
            if hh % 4 == 0:
                vaug = vreg.view("vaug", 0, (NB, 4, 128), BF16)
                kb.op(dve, lambda h: h.memset(vaug[:, :, :, 0:64], 1.0), writes=[vaug])
                vproj(C_FV + hh * 64, vaug)
            qa, ka = fbuf[hh % 2]
            bg = fox_jobs(hh + 1) if hh + 1 < 8 else []
            bgs = {"n": 0, "every": 7}
            for j in range(NT):
                acc = attend(j, causal_struct, qa[0:65, :], [qa], lambda kt: ka[0:65, kt * 128:(kt + 1) * 128], [ka],
                             lambda kt: 128, lambda kt: vaug_ap(vaug, kt, hh % 4), [vaug], 0.125,
                             bias_fn=lambda kt: csT[:, kt * 8 + hh:kt * 8 + hh + 1], bbufs=[csT], bg=bg, bgs=bgs)
                norm_out(acc, concat[(hh % 2) * 64:(hh % 2) * 64 + 64, 4 + hh // 2, tile_sl(j)], [concat])
            while bg:
                bg.pop(0)()

        mark("fox")
        vaug = vreg.view("vaug", 0, (NB, 4, 128), BF16)
        kb.op(dve, lambda h: h.memset(vaug[:, :, :, 0:64], 1.0), writes=[vaug])
        vproj(C_DV, vaug)
        dbuf = []
        for i_ in range(2):
            b0 = QO + i_ * 12288
            qd_ = big.view("dq%d" % i_, b0, (S,), BF16)
            kz0_ = big.view("dkz0%d" % i_, b0 + 4096, (S,), BF16)
            kz1_ = big.view("dkz1%d" % i_, b0 + 8192, (S,), BF16)
            kb.op(dve, lambda h: h.memset(qd_[64:128, :], 0.0), writes=[qd_])
            kb.op(dve, lambda h: h.memset(kz0_[32:64, :], 0.0), writes=[kz0_])
            kb.op(dve, lambda h: h.memset(kz0_[64:128, :], 0.0), writes=[kz0_])
            kb.op(dve, lambda h: h.memset(kz1_[64:128, :], 0.0), writes=[kz1_])
            dbuf.append((qd_, kz0_, kz1_))

        def diff_jobs(hh):
            qd, kz0, kz1 = dbuf[hh % 2]
            kd = kz1
            st = {}

            def j_acq():
                qk3 = lambda t_, o_: t_[:, o_:o_ + 1024].rearrange("p (k c) -> p k c", c=128)
                cid = ring.add(1, [((lambda t_, o_: qk3(t_, o_)[:, :, 0:64]), win[:, :, C_DQ + hh * 64:C_DQ + (hh + 1) * 64]),
                                   ((lambda t_, o_: qk3(t_, o_)[:, :, 64:128]), win[:, :, C_DK + hh * 64:C_DK + (hh + 1) * 64])])
                st["cid"] = cid
                st["acq"] = ring.acquire(cid)

            def j_proj(t):
                def fa():
                    tt, o, sbufs = st["acq"]
                    ps = proj_fm(tt, o, 128, 128, sbufs, t, 128)
                    st["t1"] = rope_a(ps, 128)
                    if t == NT - 1:
                        ring.release(st["cid"])

                def fb():
                    rope_b(st["t1"], 128, t, None, None, 32,
                           dsts=[(0, 64, qd[0:64, tile_sl(t)], [qd]), (64, 128, kd[0:64, tile_sl(t)], [kd])])
                return [fa, fb]

            def j_split():
                kb.op(dve, lambda h: h.tensor_copy(out=kz0[0:32, :], in_=kz1[0:32, :]), reads=[kz1], writes=[kz0])
                kb.op(dve, lambda h: h.memset(kz1[0:32, :], 0.0), writes=[kz1])

            jobs = [j_acq]
            for t in range(NT):
                jobs += j_proj(t)
            jobs.append(j_split)
            return jobs

        for job in diff_jobs(0):
            job()
        for hh in range(4):
            qd, kz0, kz1 = dbuf[hh % 2]
            kzs = (kz0, kz1)
            bg = diff_jobs(hh + 1) if hh + 1 < 4 else []
            bgs = {"n": 0, "every": 7}
            for j in range(NT):
                os_ = []
                for c in range(2):
                    kz = kzs[c]
                    acc = attend(j, causal_struct, qd[0:128, :], [qd],
                                 lambda kt: kz[0:128, kt * 128:(kt + 1) * 128], [kz], lambda kt: 128,
                                 lambda kt: vaug_ap(vaug, kt, hh), [vaug], 32 ** -0.5, bg=bg, bgs=bgs)
                    oc = tf()
                    norm_out(acc, oc[0:64, :], [oc])
                    os_.append(oc)
                o0, o1 = os_
                kb.op(dve, lambda h: h.scalar_tensor_tensor(out=o0[0:64, :], in0=o1[0:64, :], scalar=nlam[0:64, l:l + 1],
                                                            in1=o0[0:64, :], op0=ALU.mult, op1=ALU.add),
                      reads=[o0, o1, nlam], writes=[o0])
                sq = nextP()
                kb.op(dve, lambda h: h.tensor_tensor(out=sq[0:64, :], in0=o0[0:64, :], in1=o0[0:64, :], op=ALU.mult),
                      reads=[o0], writes=[sq])
                pm = rotX()
                kb.op(pe, lambda h: h.matmul(pm[0:64, :], lhsT=o64_b[0:64, 0:64], rhs=sq[0:64, :], start=True, stop=True),
                      reads=[sq, o64_b], writes=[pm])
                rs = o1
                kb.op(act, lambda h: h.activation(out=rs[0:64, :], in_=pm[0:64, :], func=AF.Ln, bias=eps_t[0:64, 0:1]),
                      reads=[pm, eps_t], writes=[rs])
                kb.op(act, lambda h: h.activation(out=rs[0:64, :], in_=rs[0:64, :], func=AF.Exp, scale=-0.5), reads=[rs], writes=[rs])
                kb.op(dve, lambda h: h.scalar_tensor_tensor(out=concat[(hh % 2) * 64:(hh % 2) * 64 + 64, 2 + hh // 2, tile_sl(j)],
                                                            in0=o0[0:64, :], scalar=dgs[0:64, l:l + 1], in1=rs[0:64, :],
                                                            op0=ALU.mult, op1=ALU.mult),
                      reads=[o0, rs, dgs], writes=[concat])
            while bg:
                bg.pop(0)()

        mark("diff")
        attn["on"] = False
        nsa(l, concat, win, QO)
        attn["on"] = False
        attn["imp"] = False
        mark("nsa")

        if dbg and l == 0 and si == 0:
            d3 = dbg_d["cat"].rearrange("p (c s) -> p c s", s=S)
            for c in range(KC):
                kb.out_toks.append(kb.dma(pool, d3[:, c, :], concat[:, c, :], reads=[concat]))
        wo = wview(W["w_out"][l])
        Ys = {}

        def wo_mm(t):
            ids = [ring.add(1, [(dst3(0, KC, 128), wo[:, :, m * 128:(m + 1) * 128])]) for m in range(KC)]
            if t == NT - 1:
                MIXER["ffn2_ids"] = ffn_chunks(l, 2)
            Y = big.view("ytile%d" % (t % 2), QO + (t % 2) * 16384, (KC, 512), F32)
            Ys[t] = Y
            for c in range(KC):
                kb.dma(sp, Y[:, c, :], xsp3[:, c, tile_sl(t)], reads=[xspB], writes=[Y])
            for m in range(KC):
                tt, o, sbufs = ring.acquire(ids[m])
                ps = rotB()
                for c in range(KC):
                    kb.op(pe, lambda h: h.matmul(ps[:, :], lhsT=tt[:, o + c * 128:o + (c + 1) * 128],
                                                 rhs=concat[:, c, tile_sl(t)], start=(c == 0), stop=(c == KC - 1)),
                          reads=[concat] + sbufs, writes=[ps])
                ring.release(ids[m])
                kb.op(dve, lambda h: h.tensor_tensor(out=Y[:, m, :], in0=Y[:, m, :], in1=ps[:, :], op=ALU.add),
                      reads=[Y, ps], writes=[Y])

        wo_mm(0)
        for t in range(NT):
            if t + 1 < NT:
                wo_mm(t + 1)
            Y = Ys[t]
            ln_tile(Y, slice(0, 512), l, 1, True, tile_sl(t))
            for c in range(KC):
                kb.dma(sp, xsp3[:, c, tile_sl(t)], Y[:, c, :], reads=[Y], writes=[xspB])
        x32[0] = big.view("x32", 0, (KC, S), F32)
        Xn = x32[0]
        for c in range(KC):
            kb.dma(sp, Xn[:, c, :], xsp3[:, c, :], reads=[xspB], writes=[Xn])

    def rope_cols(ps, rows, w, cos_ap, sin_ap, dst_ap, dst_bufs, rt):
        t1 = rtf()
        kb.op(act, lambda h: h.copy(t1[0:rows, 0:w], ps[0:rows, 0:w]), reads=[ps], writes=[t1])
        p2 = rotX()
        kb.op(pe, lambda h: h.matmul(p2[0:rows, 0:w], lhsT=rt[0:rows, 0:rows], rhs=t1[0:rows, 0:w], start=True, stop=True),
              reads=[t1, rt], writes=[p2])
        t2 = rtf()
        kb.op(dve, lambda h: h.tensor_tensor(out=t2[0:rows, 0:w], in0=p2[0:rows, 0:w], in1=sin_ap, op=ALU.mult),
              reads=[p2, cF["sin64"]], writes=[t2])
        kb.op(dve, lambda h: h.tensor_tensor(out=t1[0:rows, 0:w], in0=t1[0:rows, 0:w], in1=cos_ap, op=ALU.mult),
              reads=[t1, cF["cos64"]], writes=[t1])
        kb.op(dve, lambda h: h.tensor_tensor(out=dst_ap, in0=t1[0:rows, 0:w], in1=t2[0:rows, 0:w], op=ALU.add),
              reads=[t1, t2], writes=list(dst_bufs))

    def nsa(l, concat, win, QO):
        qn = [big.view(f"nq{h_}", QO + h_ * 4096, (S,), BF16) for h_ in range(4)]
        kslc = big.view("nkslc", QO + 16384, (S,), BF16)
        kwin = big.view("nkwin", QO + 20480, (S,), BF16)
        gT = big.view("ngT", QO + 24576, (S,), BF16)
        selT = big.view("nselT", QO + 28672, (1024,), BF16)
        kcT = big.view("nkcT", QO + 30720, (128,), BF16)
        vca = big.view("nvca", QO + 30976, (64,), BF16)
        kcmp = vreg.view("nkcmp", 0, (S,), BF16)
        vcmp = vreg.view("nvcmp", 4096, (S,), BF16)
        blk = vreg.view("nblk", 8192, (32, 128), BF16)
        c64, s64, r64 = cF["cos64"], cF["sin64"], cF["rt64"]
        for b_ in qn + [kslc, kwin]:
            kb.op(dve, lambda h: h.memset(b_[64:128, :], 0.0), writes=[b_])
        kb.op(dve, lambda h: h.memset(selT[:, :], 0.0), writes=[selT])
        cid = ring.add(1, [(dst3(0, KC, 256), win[:, :, 0:256])])
        tt, o, sbufs = ring.acquire(cid)
        for p_ in range(2):
            for t in range(NT):
                ps = proj_fm(tt, o + p_ * 128, 128, 256, sbufs, t, 128)
                t1 = rope_a(ps, 128)
                rope_b(t1, 128, t, None, None, 64,
                       dsts=[(0, 64, qn[2 * p_][0:64, tile_sl(t)], [qn[2 * p_]]),
                             (64, 128, qn[2 * p_ + 1][0:64, tile_sl(t)], [qn[2 * p_ + 1]])])
        ring.release(cid)
        cid = ring.add(1, [(lambda t_, o_: t_[:, o_:o_ + 2048].rearrange("p (k c) -> p k c", c=256)[:, :, 0:64], win[:, :, C_KCMP:C_KCMP + 64]),
                           (lambda t_, o_: t_[:, o_:o_ + 2048].rearrange("p (k c) -> p k c", c=256)[:, :, 64:128], win[:, :, C_VCMP:C_VCMP + 64]),
                           (lambda t_, o_: t_[:, o_:o_ + 2048].rearrange("p (k c) -> p k c", c=256)[:, :, 128:192], win[:, :, C_KSLC:C_KSLC + 64]),
                           (lambda t_, o_: t_[:, o_:o_ + 2048].rearrange("p (k c) -> p k c", c=256)[:, :, 192:256], win[:, :, C_KWIN:C_KWIN + 64]),
                           (dst3(2048, KC, 12), win[:, :, C_NG:C_NG + 12])])
        tt, o, sbufs = ring.acquire(cid)
        for t in range(NT):
            ps = proj_fm(tt, o, 128, 256, sbufs, t, 128)
            kb.op(act, lambda h: h.copy(kcmp[0:64, tile_sl(t)], ps[0:64, :]), reads=[ps], writes=[kcmp])
            kb.op(act, lambda h: h.copy(vcmp[0:64, tile_sl(t)], ps[64:128, :]), reads=[ps], writes=[vcmp])
            ps = proj_fm(tt, o + 128, 128, 256, sbufs, t, 128)
            t1 = rope_a(ps, 128)
            rope_b(t1, 128, t, None, None, 64,
                   dsts=[(0, 64, kslc[0:64, tile_sl(t)], [kslc]), (64, 128, kwin[0:64, tile_sl(t)], [kwin])])
            ps = proj_fm(tt, o + 2048, 12, 12, sbufs, t, 12)
            kb.op(act, lambda h: h.activation(out=gT[0:12, tile_sl(t)], in_=ps[0:12, :], func=AF.Sigmoid),
                  reads=[ps], writes=[gT])
        ring.release(cid)
        mark("n_proj")
        for kv, (src, p1n, p2n) in enumerate([(kcmp, "nsa_phi_k1", "nsa_phi_k2"), (vcmp, "nsa_phi_v1", "nsa_phi_v2")]):
            for a in range(32):
                sap = mkap(src[0:64, a:a + 1], 0, [(16, 127)])
                kb.op(dve, lambda h: h.tensor_scalar(out=blk[0:64, a, 0:127], in0=sap,
                                                     scalar1=peT[0:64, (l * 2 + kv) * 32 + a:(l * 2 + kv) * 32 + a + 1],
                                                     scalar2=None, op0=ALU.add), reads=[src, peT], writes=[blk])
            p1src = W[p1n][l].rearrange("(a d) j -> d a j", d=64)
            c1 = ring.add(3, [((lambda t_, o_, q_=q_: t_[0:64, o_ + q_ * 2048:o_ + (q_ + 1) * 2048].rearrange("p (a j) -> p a j", j=256)),
                               p1src[:, q_ * 8:(q_ + 1) * 8, :]) for q_ in range(4)])
            c2 = ring.add(1, [(lambda t_, o_: t_[:, o_:o_ + 128].rearrange("p (c d) -> p c d", d=64),
                               W[p2n][l].rearrange("(c p) d -> p c d", p=128))])
            tt, o, sbufs = ring.acquire(c1)
            G = []
            for jc in range(2):
                ps = rotA()
                for a in range(32):
                    kb.op(pe, lambda h: h.matmul(ps[:, 0:127], lhsT=tt[0:64, o + a * 256 + jc * 128:o + a * 256 + (jc + 1) * 128],
                                                 rhs=blk[0:64, a, 0:127], start=(a == 0), stop=(a == 31)),
                          reads=[blk] + sbufs, writes=[ps])
                u = tf()
                kb.op(dve, lambda h: h.tensor_tensor(out=u[:, 0:127], in0=ps[:, 0:127], in1=ps[:, 0:127], op=ALU.mult) if False else
                      h.tensor_copy(out=u[:, 0:127], in_=ps[:, 0:127]), reads=[ps], writes=[u])
                v = tf()
                kb.op(dve, lambda h: h.tensor_tensor(out=v[:, 0:127], in0=u[:, 0:127], in1=u[:, 0:127], op=ALU.mult),
                      reads=[u], writes=[v])
                kb.op(dve, lambda h: h.tensor_scalar(out=v[:, 0:127], in0=v[:, 0:127], scalar1=0.044715, scalar2=1.0,
                                                     op0=ALU.mult, op1=ALU.add), reads=[v], writes=[v])
                kb.op(dve, lambda h: h.tensor_tensor(out=v[:, 0:127], in0=v[:, 0:127], in1=u[:, 0:127], op=ALU.mult),
                      reads=[u, v], writes=[v])
                kb.op(act, lambda h: h.activation(out=v[:, 0:127], in_=v[:, 0:127], func=AF.Tanh, scale=0.7978845608028654),
                      reads=[v], writes=[v])
                kb.op(dve, lambda h: h.tensor_scalar(out=v[:, 0:127], in0=v[:, 0:127], scalar1=1.0, scalar2=0.5,
                                                     op0=ALU.add, op1=ALU.mult), reads=[v], writes=[v])
                g_ = htiles[jc]
                kb.op(dve, lambda h: h.tensor_tensor(out=g_[:, 0:127], in0=v[:, 0:127], in1=u[:, 0:127], op=ALU.mult),
                      reads=[u, v], writes=[g_])
                G.append(g_)
            ring.release(c1)
            tt, o, sbufs = ring.acquire(c2)
            ps = rotA()
            if kv == 0:
                for jc in range(2):
                    kb.op(pe, lambda h: h.matmul(ps[0:64, 0:127], lhsT=tt[:, o + jc * 64:o + (jc + 1) * 64], rhs=G[jc][:, 0:127],
                                                 start=(jc == 0), stop=(jc == 1)), reads=G + sbufs, writes=[ps])
                rope_cols(ps, 64, 127, mkap(c64[0:64, 31:32], 0, [(16, 127)]), mkap(s64[0:64, 31:32], 0, [(16, 127)]),
                          kcT[0:64, 0:127], [kcT], r64)
            else:
                for jc in range(2):
                    kb.op(pe, lambda h: h.matmul(ps[0:127, 0:64], lhsT=G[jc][:, 0:127], rhs=tt[:, o + jc * 64:o + (jc + 1) * 64],
                                                 start=(jc == 0), stop=(jc == 1)), reads=G + sbufs, writes=[ps])
                kb.op(act, lambda h: h.copy(vca[0:127, 0:64], ps[0:127, 0:64]), reads=[ps], writes=[vca])
            ring.release(c2)
        mark("n_cmpr")
        vaug = vreg.view("vaug", 0, (NB, 2, 128), BF16)
        kb.op(dve, lambda h: h.memset(vaug[:, :, :, 0:64], 1.0), writes=[vaug])
        cid = ring.add(1, [(lambda t_, o_: t_[:, o_:o_ + 1024].rearrange("p (k c) -> p k c", c=128)[:, :, 0:64], win[:, :, C_VSLC:C_VSLC + 64]),
                           (lambda t_, o_: t_[:, o_:o_ + 1024].rearrange("p (k c) -> p k c", c=128)[:, :, 64:128], win[:, :, C_VWIN:C_VWIN + 64])])
        tt, o, sbufs = ring.acquire(cid)
        for b in range(NB):
            ps = rotA()
            for k in range(KC):
                kb.op(pe, lambda h: h.matmul(ps[:, 0:128], lhsT=xb[:, k, b * 128:(b + 1) * 128], rhs=tt[:, o + k * 128:o + (k + 1) * 128],
                                             start=(k == 0), stop=(k == KC - 1)), reads=[xb] + sbufs, writes=[ps])
            kb.op(act, lambda h: h.copy(vaug[:, b, :, 64:128], ps[:, 0:128].rearrange("p (h d) -> p h d", d=64)),
                  reads=[ps], writes=[vaug])
        ring.release(cid)

        def grep_(h_, br, t):
            pg = rotX()
            kb.op(pe, lambda h: h.matmul(pg[64:128, :], lhsT=cF["oh"][0:12, (h_ * 3 + br) * 64:(h_ * 3 + br + 1) * 64],
                                         rhs=gT[0:12, tile_sl(t)], start=True, stop=True), reads=[gT, cF["oh"]], writes=[pg])
            return pg

        kb.op(pe, lambda h: h.matmul(bank_imp[:, :], lhsT=zer_b[:, 0:128], rhs=zer_b[:, :], start=True, stop=False),
              reads=[zer_b], writes=[bank_imp])
        for h_ in range(4):
            for j in range(NT):
                pss = rotA()
                kb.op(pe, lambda h: h.matmul(pss[0:127, :], lhsT=kcT[0:64, 0:127], rhs=qn[h_][0:64, tile_sl(j)], start=True, stop=True),
                      reads=[kcT, qn[h_]], writes=[pss])
                P = nextP()
                kb.op(act, lambda h: h.activation(out=P[0:127, :], in_=pss[0:127, :], func=AF.Exp, scale=0.125), reads=[pss], writes=[P])
                kb.op(dve, lambda h: h.tensor_tensor(out=P[0:127, :], in0=P[0:127, :], in1=cF["cmpmask"][0:127, tile_sl(j)], op=ALU.mult),
                      reads=[P, cF["cmpmask"]], writes=[P])
                pr = rotA()
                kb.op(pe, lambda h: h.matmul(pr[:, :], lhsT=ones_b[0:127, :], rhs=P[0:127, :], start=True, stop=True),
                      reads=[P, ones_b], writes=[pr])
                r = tf()
                kb.op(dve, lambda h: h.tensor_scalar(out=r[:, :], in0=pr[:, :], scalar1=1e-30, scalar2=None, op0=ALU.max), reads=[pr], writes=[r])
                recip_act(r[:, :], [r])
                Pn = nextP()
                kb.op(dve, lambda h: h.tensor_tensor(out=Pn[0:127, :], in0=P[0:127, :], in1=r[0:127, :], op=ALU.mult), reads=[P, r], writes=[Pn])
                po = rotA()
                kb.op(pe, lambda h: h.matmul(po[64:128, :], lhsT=vca[0:127, 0:64], rhs=Pn[0:127, :], start=True, stop=True),
                      reads=[Pn, vca], writes=[po])
                for qb in range(4):
                    b = 4 * j + qb
                    kb.op(pe, lambda h: h.matmul(bank_imp[:, b * 32:(b + 1) * 32], lhsT=Pn[0:127, qb * 128:(qb + 1) * 128],
                                                 rhs=cF["mcs"][0:127, :], start=False, stop=(h_ == 3 and b == NB - 1)),
                          reads=[Pn, cF["mcs"]], writes=[bank_imp])
                pg = grep_(h_, 0, j)
                oc = tf()
                kb.op(dve, lambda h: h.tensor_copy(out=oc[64:128, :], in_=po[64:128, :]), reads=[po], writes=[oc])
                kb.op(dve, lambda h: h.tensor_tensor(out=concat[(h_ % 2) * 64:(h_ % 2) * 64 + 64, h_ // 2, tile_sl(j)],
                                                     in0=oc[64:128, :], in1=pg[64:128, :], op=ALU.mult), reads=[oc, pg], writes=[concat])
        mark("n_cmpat")
        for b in range(8, NB):
            sc = tf()
            kb.op(dve, lambda h: h.tensor_tensor(out=sc[:, 0:32], in0=bank_imp[:, b * 32:(b + 1) * 32], in1=cF["keep"][:, b * 32:(b + 1) * 32],
                                                 op=ALU.mult), reads=[bank_imp, cF["keep"]], writes=[sc])
            kb.op(dve, lambda h: h.tensor_tensor(out=sc[:, 0:32], in0=sc[:, 0:32], in1=cF["addt"][:, b * 32:(b + 1) * 32], op=ALU.add),
                  reads=[sc, cF["addt"]], writes=[sc])
            kb.op(dve, lambda h: h.max(out=sc[:, 64:72], in_=sc[:, 0:32]), reads=[sc], writes=[sc])
            kb.op(dve, lambda h: h.match_replace(out=sc[:, 32:64], in_to_replace=sc[:, 64:72], in_values=sc[:, 0:32], imm_value=-1e30),
                  reads=[sc], writes=[sc])
            kb.op(dve, lambda h: h.max(out=sc[:, 72:80], in_=sc[:, 32:64]), reads=[sc], writes=[sc])
            kb.op(dve, lambda h: h.tensor_scalar(out=sc[:, 128:160], in0=sc[:, 0:32], scalar1=sc[:, 79:80], scalar2=None, op0=ALU.is_ge),
                  reads=[sc], writes=[sc])
            kb.op(dve, lambda h: h.tensor_scalar(out=sc[:, 128:160], in0=sc[:, 128:160], scalar1=30000.0, scalar2=-30000.0,
                                                 op0=ALU.mult, op1=ALU.add), reads=[sc], writes=[sc])
            pt_ = rotA()
            kb.op(pe, lambda h: h.transpose(pt_[0:32, 0:128], sc[:, 128:160], cF["ident"][:]), reads=[sc, cF["ident"]], writes=[pt_])
            kb.op(act, lambda h: h.copy(selT[0:32, (b - 8) * 128:(b - 7) * 128], pt_[0:32, 0:128]), reads=[pt_], writes=[selT])

        mark("n_topk")

        def slc_smask(kt, j, c0, c1):
            if j < 2:
                return None
            return (cF["emat"][0:128, kt * 128:(kt + 1) * 128],
                    selT[0:128, (j - 2) * 512 + c0:(j - 2) * 512 + c1], [selT, cF["emat"]])

        attn["on"] = True
        attn["imp"] = True
        for h_ in range(4):
            for j in range(NT):
                for br, (kt_, struct, vi, smk) in enumerate([(kslc, causal_struct, 0, slc_smask), (kwin, window_struct, 1, None)]):
                    acc = attend(j, struct, qn[h_][0:128, :], [qn[h_]], lambda kt: kt_[0:128, kt * 128:(kt + 1) * 128], [kt_],
                                 lambda kt: 128, lambda kt: vaug[:, kt, vi, :], [vaug], 0.125, smask=smk)
                    pg = grep_(h_, br + 1, j)
                    ob = tf()
                    gsb = tf()
                    kb.op(act, lambda h: h.copy(gsb[64:128, :], pg[64:128, :]), reads=[pg], writes=[gsb])
                    p0 = (h_ % 2) * 64
                    norm_out(acc, ob[p0:p0 + 64, :], [ob], mul_ap=gsb[64:128, :], mul_bufs=[gsb])
                    dst = concat[p0:p0 + 64, h_ // 2, tile_sl(j)]
                    kb.op(dve, lambda h: h.tensor_tensor(out=dst, in0=dst, in1=ob[p0:p0 + 64, :], op=ALU.add), reads=[ob, concat], writes=[concat])

    MIXER["fn"] = mixer

    marks = []
    kb.marks = marks

    def mark(name):
        marks.append((name, kb.pe.cnt))

    def run():
        for si in range(nseq):
            if "noload" not in SKIP:
                load_x(si)
            else:
                x32[0] = big.view("x32", 0, (KC, S), F32)
                kb.op(dve, lambda h: h.memset(x32[0][:, 0, :], 1.0), writes=[x32[0]])
            if stop_after == "load":
                store_out(si, 1.0 / ALPHA)
                return
            if stop_after == "ffnonly":
                ffn(ffn_chunks(0, 1))
                store_out(si, 1.0 / ALPHA)
                return
            for l in range(depth):
                last = (l == depth - 1)
                mark("start")
                ffn(MIXER.pop("ffn1_ids") if "ffn1_ids" in MIXER else ffn_chunks(l, 1))
                mark("ffn1")
                layernorm(l, 0, True)
                mark("ln1")
                if stop_after == "ffn1":
                    store_out(si, 1.0 / ALPHA)
                    return
                if "fn" in MIXER:
                    MIXER["fn"](l, si)
                    if stop_after == "mixer":
                        store_out(si, 1.0 / ALPHA)
                        return
                mark("mixer")
                ffn(MIXER.pop("ffn2_ids") if "ffn2_ids" in MIXER else ffn_chunks(l, 2))
                mark("ffn2")
                layernorm(l, 2, False)
                mark("ln3")
                nxt = (l + 1) if not last else (0 if si + 1 < nseq else None)
                ple(l, si, last, nxt)
                mark("ple")
            store_out(si)
            mark("store")

    print('sbuf bytes remaining', nc.sbuf_bytes_remaining)
    return nc, es, kb, run, locals()


def finish(nc, es, kb):
    for tok in kb.out_toks:
        kb.wait_tok(kb.sp, tok)
    es.close()
    return nc


def build_program(nseq=SEQ_PER_CORE, depth=DEPTH, stop_after=None):
    nc, es, kb, run, env = build(nseq, depth, stop_after)
    add_mixer(env)
    run()
    return finish(nc, es, kb)


def add_mixer(env):
    pass


_CONSTS = None


def kernel(**inputs):
    global _CONSTS
    if _CONSTS is None:
        _CONSTS = make_consts()
    nc = build_program()
    x = np.ascontiguousarray(inputs["x"], dtype=np.float32)
    p = np.ascontiguousarray(inputs["p"], dtype=np.float32)
    in_maps = []
    for c in range(NCORES):
        m = {"x": x[c * SEQ_PER_CORE:(c + 1) * SEQ_PER_CORE],
             "p": np.ascontiguousarray(p[:, c * SEQ_PER_CORE:(c + 1) * SEQ_PER_CORE])}
        for n in WEIGHT_SHAPES:
            m[n] = np.ascontiguousarray(inputs[n], dtype=np.float32)
        for n in CONST_SHAPES:
            m["c_" + n] = _CONSTS[n]
        in_maps.append(m)
    res = run_bass_kernel_spmd(nc, in_maps, core_ids=list(range(NCORES)))
    return np.concatenate([r["out"] for r in res.results], axis=0)
```

```python
import math
import contextlib
import numpy as np
import concourse.bass as bass
import concourse.mybir as mybir
from concourse.bass_utils import run_bass_kernel_spmd

F32 = mybir.dt.float32
BF16 = mybir.dt.bfloat16
AF = mybir.ActivationFunctionType
ALU = mybir.AluOpType
AX = mybir.AxisListType

D = 1024
S = 2048
KC = 8
NT = 4
NB = 16
DFF = 2752
NF = 22
INC = 2964
PLE = 256
DEPTH = 2
ALPHA = (2.0 * DEPTH) ** 0.25
EPS = 1e-5
SLOT = 3072
NSLOT = 6
ROLL = 10 ** 9
NCORES = 8
SEQ_PER_CORE = 4

C_NQ = 0
C_KCMP, C_VCMP, C_KSLC, C_VSLC, C_KWIN, C_VWIN = 256, 320, 384, 448, 512, 576
C_NG = 640
C_DQ, C_DK, C_DV = 652, 908, 1164
C_FQ, C_FK, C_FV, C_FF = 1420, 1932, 2444, 2956


class Buf:
    def __init__(self, ap, name=""):
        self.ap = ap
        self.w = {}
        self.r = {}
        self.name = name
        self.dead = False
        self.rng = None
        self.excl = False

    def __getitem__(self, idx):
        return self.ap[idx]


class Eng:
    def __init__(self, h, name):
        self.h = h
        self.name = name
        self.sem = None
        self.cnt = 0
        self.seen = {}
        self.nsem = 0


class KB:
    def __init__(self, nc, es):
        self.nc = nc
        self.es = es
        self.pe = Eng(nc.tensor, "pe")
        self.act = Eng(nc.scalar, "act")
        self.dve = Eng(nc.vector, "dve")
        self.pool = Eng(nc.gpsimd, "pool")
        self.sp = Eng(nc.sync, "sp")
        for e in (self.pe, self.act, self.dve, self.pool, self.sp):
            self._newsem(e)
        self.dq = {}
        self.out_toks = []

    def _newsem(self, e):
        e.sem = self.es.enter_context(self.nc.semaphore(f"s_{e.name}_{e.nsem}"))
        e.nsem += 1
        e.cnt = 0

    def _wait(self, e, tok):
        sem, val, owner = tok
        if owner is e and e.name == "pe":
            return
        k = id(sem)
        if e.seen.get(k, 0) >= val:
            return
        e.h.wait_ge(sem, val)
        e.seen[k] = val

    def _deps(self, e, reads, writes):
        need = {}

        def add(t):
            k = id(t[0])
            if k not in need or need[k][1] < t[1]:
                need[k] = t

        for b in reads:
            assert not b.dead, b.name
            for t in b.w.values():
                add(t)
            if b.excl:
                for t in b.r.values():
                    add(t)
        for b in writes:
            assert not b.dead, b.name
            for t in b.w.values():
                add(t)
            for t in b.r.values():
                add(t)
        for t in need.values():
            self._wait(e, t)

    def _upd(self, tok, reads, writes, is_dma=False):
        k = id(tok[0])
        for b in reads:
            old = b.r.get(k)
            if old is None or old[1] < tok[1]:
                b.r[k] = tok
        for b in writes:
            if is_dma:
                b.w[k] = tok
            else:
                b.w = {k: tok}
            b.r = {}

    def op(self, e, fn, reads=(), writes=()):
        self._deps(e, reads, writes)
        if e.cnt >= ROLL:
            self._newsem(e)
        ins = fn(e.h)
        e.cnt += 1
        ins.then_inc(e.sem, 1)
        tok = (e.sem, e.cnt, e)
        self._upd(tok, reads, writes)
        return tok

    def dma(self, e, out_ap, in_ap, reads=(), writes=()):
        self._deps(e, reads, writes)
        q = self.dq.setdefault(e.name, {"sems": [], "i": 0})
        K = 4 if e.name == "pool" else 8
        i = q["i"]
        q["i"] += 1
        if len(q["sems"]) < K:
            q["sems"].append([self.es.enter_context(self.nc.semaphore(f"d_{e.name}_{len(q['sems'])}")), 0])
        slot = q["sems"][i % K]
        sem, uses = slot
        if uses > 0:
            self._wait(e, (sem, 16 * uses, None))
        ins = e.h.dma_start(out=out_ap, in_=in_ap, allow_slow_non_contiguous=True)
        ins.then_inc(sem, 16)
        slot[1] = uses + 1
        tok = (sem, 16 * (uses + 1), None)
        self._upd(tok, reads, writes, is_dma=True)
        return tok

    def wait_tok(self, e, tok):
        self._wait(e, tok)


class Region:
    def __init__(self, kb, name, nbytes):
        self.kb = kb
        self.t = kb.es.enter_context(kb.nc.sbuf_tensor(name, [128, nbytes // 2], BF16))
        self.nbytes = nbytes
        self.live = []
        self.name = name

    def view(self, name, off, shape, dtype, parts=128):
        esz = 4 if dtype == F32 else 2
        n = int(np.prod(shape))
        nb = n * esz
        assert off % 4 == 0 and off + nb <= self.nbytes, (name, off, nb, self.nbytes)
        ap = self.t[0:parts, off // 2:(off + nb) // 2]
        if dtype == F32:
            ap = ap.bitcast(F32)
        if len(shape) == 2:
            ap = ap.rearrange("p (a b) -> p a b", b=shape[1])
        elif len(shape) == 3:
            ap = ap.rearrange("p (a b c) -> p a b c", b=shape[1], c=shape[2])
        b = Buf(ap, name)
        b.rng = (off, off + nb)
        keep = []
        for o in self.live:
            if o.rng[0] < b.rng[1] and b.rng[0] < o.rng[1]:
                o.dead = True
                for k, t in list(o.w.items()) + list(o.r.items()):
                    if k not in b.r or b.r[k][1] < t[1]:
                        b.r[k] = t
                if not (b.rng[0] <= o.rng[0] and o.rng[1] <= b.rng[1]):
                    keep.append(o)
            else:
                keep.append(o)
        keep.append(b)
        self.live = keep
        return b


class Ring:
    def __init__(self, kb, nslot):
        self.kb = kb
        self.nslot = nslot
        self.t = kb.es.enter_context(kb.nc.sbuf_tensor("ring", [128, nslot * SLOT], BF16))
        self.slots = [Buf(self.t[:, i * SLOT:(i + 1) * SLOT], f"slot{i}") for i in range(nslot)]
        self.owner = [None] * nslot
        self.chunks = []
        self.issued = 0
        self.cursor = 0

    def add(self, nsl, dmas):
        self.chunks.append({"n": nsl, "dmas": dmas, "start": None, "done": False})
        return len(self.chunks) - 1

    def _try_issue(self):
        c = self.chunks[self.issued]
        cur = self.cursor
        if cur + c["n"] > self.nslot:
            cur = 0
        for s in range(cur, cur + c["n"]):
            o = self.owner[s]
            if o is not None and not self.chunks[o]["done"]:
                return False
        c["start"] = cur
        bufs = self.slots[cur:cur + c["n"]]
        for (dst_fn, src) in c["dmas"]:
            dst = dst_fn(self.t, cur * SLOT)
            self.kb.dma(self.kb.pool, dst, src, writes=bufs)
        for s in range(cur, cur + c["n"]):
            self.owner[s] = self.issued
        self.cursor = cur + c["n"]
        self.issued += 1
        return True

    def acquire(self, i):
        while self.issued < len(self.chunks):
            if not self._try_issue():
                break
        c = self.chunks[i]
        assert c["start"] is not None, f"ring too small for chunk {i}"
        st = c["start"]
        return self.t, st * SLOT, self.slots[st:st + c["n"]]

    def release(self, i):
        self.chunks[i]["done"] = True


def mkap(base, off, dims):
    p = base.ap[0]
    return bass.AP(tensor=base.tensor, offset=base.offset + off, ap=[[p[0], p[1]]] + [[s, c] for s, c in dims])


def make_consts():
    c = {}
    c["ident"] = np.eye(128, dtype=np.float32)
    k = np.arange(128)[:, None]
    q = np.arange(128)[None, :]
    c["tri"] = np.where(k <= q, 0.0, -30000.0).astype(np.float32)
    c["anti"] = np.where(k > q, 0.0, -30000.0).astype(np.float32)
    c["identb"] = np.eye(128, dtype=np.float32)
    pos = np.arange(S, dtype=np.float32)

    def rope_tab(d):
        half = d // 2
        inv = (10000.0 ** (-np.arange(0, d, 2, dtype=np.float32) / d)).astype(np.float32)
        ang = pos[None, :] * inv[:, None]
        r = np.arange(128) % half
        return np.cos(ang)[r].astype(np.float32), np.sin(ang)[r].astype(np.float32)

    c["cos64"], c["sin64"] = rope_tab(64)
    c["cos32"], c["sin32"] = rope_tab(32)

    def rot_T(d):
        half = d // 2
        R = np.zeros((128, 128), np.float32)
        for m in range(128):
            g, i = divmod(m, d)
            if i < half:
                R[m, g * d + i + half] = -1.0
            else:
                R[m, g * d + i - half] = 1.0
        return np.ascontiguousarray(R.T)

    c["rt64"] = rot_T(64)
    c["rt32"] = rot_T(32)
    ncmp = 127
    cm = np.zeros((128, S), np.float32)
    cm[:ncmp] = ((np.arange(ncmp) * 16 + 31)[:, None] <= np.arange(S)[None, :]).astype(np.float32)
    c["cmpmask"] = cm
    c0 = np.arange(ncmp) * 16
    s0 = np.arange(32) * 64
    m = ((c0[:, None] < s0[None, :] + 64) & (c0[:, None] + 32 > s0[None, :])).astype(np.float32)
    mc = np.zeros((128, 32), np.float32)
    mc[:ncmp] = m
    c["mcs"] = mc
    E = np.zeros((128, S), np.float32)
    E[np.arange(S) // 64, np.arange(S)] = 1.0
    c["emat"] = E
    oh = np.zeros((128, 12, 64), np.float32)
    for r in range(12):
        oh[r, r, :] = 1.0
    c["oh"] = oh.reshape(128, 768)
    keep = np.zeros((128, 16, 32), np.float32)
    add = np.zeros((128, 16, 32), np.float32)
    for b in range(16):
        t = b * 128 + np.arange(128)
        j = np.arange(32)[None, :]
        blk = (t // 64)[:, None]
        forced0 = (j == 0)
        forced1 = (j == blk)
        forced2 = (j == blk - 1)
        forced = forced0 | forced1 | forced2
        valid = (j * 64 <= t[:, None])
        keep[:, b, :] = (valid & ~forced).astype(np.float32)
        a = np.where(valid, 0.0, -1.0)
        a = np.where(forced2, 1e9, a)
        a = np.where(forced1, 2e9, a)
        a = np.where(forced0 & np.ones_like(forced1), 3e9, a)
        add[:, b, :] = a
    c["keep"] = keep.reshape(128, 512)
    c["addt"] = add.reshape(128, 512)
    return c


CONST_SHAPES = {"identb": (128, 128), "ident": (128, 128), "tri": (128, 128), "anti": (128, 128), "cos64": (128, S), "sin64": (128, S),
                "cos32": (128, S), "sin32": (128, S), "rt64": (128, 128), "rt32": (128, 128), "cmpmask": (128, S),
                "mcs": (128, 32), "emat": (128, S), "oh": (128, 768), "keep": (128, 512), "addt": (128, 512)}

WEIGHT_SHAPES = {
    "ln_g": (DEPTH, 3, D), "ln_b": (DEPTH, 3, D),
    "ffn1_w_gate": (DEPTH, D, DFF), "ffn1_w_up": (DEPTH, D, DFF), "ffn1_w_down": (DEPTH, DFF, D),
    "ffn2_w_gate": (DEPTH, D, DFF), "ffn2_w_up": (DEPTH, D, DFF), "ffn2_w_down": (DEPTH, DFF, D),
    "w_in": (DEPTH, D, INC), "fox_b_f": (DEPTH, 8), "nsa_pos_k": (DEPTH, 32, 64), "nsa_pos_v": (DEPTH, 32, 64),
    "nsa_phi_k1": (DEPTH, 2048, 256), "nsa_phi_k2": (DEPTH, 256, 64),
    "nsa_phi_v1": (DEPTH, 2048, 256), "nsa_phi_v2": (DEPTH, 256, 64),
    "diff_lambda": (DEPTH, 4, 32), "diff_subln_g": (DEPTH, 64), "w_out": (DEPTH, D, D),
    "ple_w_gate": (DEPTH, D, D), "ple_b_gate": (DEPTH, D), "ple_w_proj": (DEPTH, PLE, D),
}


def build(nseq=SEQ_PER_CORE, depth=DEPTH, stop_after=None, dbg=()):
    nc = bass.Bass("TRN2", target_bir_lowering=False)
    es = contextlib.ExitStack()
    kb = KB(nc, es)
    pe, act, dve, pool, sp = kb.pe, kb.act, kb.dve, kb.pool, kb.sp

    x_d = nc.dram_tensor("x", [nseq, S, D], F32, kind="ExternalInput").ap()
    p_d = nc.dram_tensor("p", [DEPTH, nseq, S, PLE], F32, kind="ExternalInput").ap()
    W = {n: nc.dram_tensor(n, list(s), F32, kind="ExternalInput").ap() for n, s in WEIGHT_SHAPES.items()}
    Cd = {n: nc.dram_tensor("c_" + n, list(s), F32, kind="ExternalInput").ap() for n, s in CONST_SHAPES.items()}
    out_d = nc.dram_tensor("out", [nseq, S, D], F32, kind="ExternalOutput").ap()
    xsp_d = nc.dram_tensor("xsp", [128, KC * S], F32, kind="Internal").ap()
    dbg_d = {}
    if dbg:
        dbg_d["cat"] = nc.dram_tensor("dbg_cat", [128, KC * S], F32, kind="ExternalOutput").ap()
        dbg_d["qk"] = nc.dram_tensor("dbg_qk", [128, 2 * S], F32, kind="ExternalOutput").ap()

    def sb(name, shape, dt):
        return Buf(es.enter_context(nc.sbuf_tensor(name, list(shape), dt))[:], name)

    banks = [Buf(es.enter_context(nc.psum_tensor(f"ps{i}", [128, 512], F32))[:], f"ps{i}") for i in range(8)]
    for b_ in banks:
        b_.excl = True
    rot = {"A": 0, "B": 0, "P": 0, "T": 0, "H": 0, "R": 0}

    def rotA():
        rot["A"] = (rot["A"] + 1) % 4
        return banks[rot["A"]]

    def rotB():
        rot["B"] = (rot["B"] + 1) % 3
        return banks[4 + rot["B"]]

    bank_imp = banks[7]
    attn = {"on": False, "imp": False}
    rot["S"] = 0
    rot["M"] = 0

    def rotS():
        rot["S"] = (rot["S"] + 1) % 3
        return banks[rot["S"]]

    def rotX():
        if not attn["on"]:
            return rotA()
        if attn["imp"]:
            return banks[3]
        rot["M"] ^= 1
        return banks[3] if rot["M"] else banks[7]

    big = Region(kb, "big", 65536)
    xb = sb("xb", [128, KC, S], BF16)
    ring = Ring(kb, NSLOT)
    vreg = Region(kb, "vreg", 16384)
    ptiles = [sb(f"pt{i}", [128, 512], BF16) for i in range(4)]
    tmps = [sb(f"tf{i}", [128, 512], F32) for i in range(6)]
    htiles = [sb(f"ht{i}", [128, 512], BF16) for i in range(4)]
    rtmps = [sb(f"rt{i}", [128, 512], F32) for i in range(2)]
    rot["R"] = 0

    def rtf():
        rot["R"] = (rot["R"] + 1) % 2
        return rtmps[rot["R"]]

    def nextP():
        rot["P"] = (rot["P"] + 1) % 4
        return ptiles[rot["P"]]

    def tf():
        rot["T"] = (rot["T"] + 1) % 6
        return tmps[rot["T"]]

    def nextH():
        rot["H"] = (rot["H"] + 1) % 4
        return htiles[rot["H"]]

    import os
    SKIP = os.environ.get("KSKIP", "").split(",")
    cF = {}
    for n in ("ident", "rt64", "rt32"):
        cF[n] = sb("k_" + n, CONST_SHAPES[n], F32)
        kb.dma(sp, cF[n][:], Cd[n], writes=[cF[n]])
    for n in (("identb", "tri", "anti", "cos64", "sin64", "cos32", "sin32", "cmpmask", "mcs", "emat", "oh", "keep", "addt") if "cb" not in SKIP else ()):
        cF[n] = sb("k_" + n, CONST_SHAPES[n], BF16)
        kb.dma(pool, cF[n][:], Cd[n], writes=[cF[n]])
    ones_b = sb("ones_b", [128, 128], BF16)
    kb.op(dve, lambda h: h.memset(ones_b[:], 1.0), writes=[ones_b])
    oD_b = sb("oD_b", [128, 128], BF16)
    kb.op(dve, lambda h: h.memset(oD_b[:], 1.0 / D), writes=[oD_b])
    o64_b = sb("o64_b", [128, 64], BF16)
    kb.op(dve, lambda h: h.memset(o64_b[:], 1.0 / 64.0), writes=[o64_b])
    eps_t = sb("eps_t", [128, 1], F32)
    kb.op(dve, lambda h: h.memset(eps_t[:], EPS), writes=[eps_t])
    zer_b = sb("zer_b", [128, 512], BF16)
    kb.op(dve, lambda h: h.memset(zer_b[:], 0.0), writes=[zer_b])

    lng = sb("lng", [128, DEPTH * 3 * KC], F32)
    lnb = sb("lnb", [128, DEPTH * 3 * KC], F32)
    lngs = sb("lngs", [128, DEPTH * 3 * KC], F32)
    lnbs = sb("lnbs", [128, DEPTH * 3 * KC], F32)
    pleb = sb("pleb", [128, DEPTH * KC], F32)
    for l in range(DEPTH if "ln" not in SKIP else 0):
        for i in range(3):
            o = (l * 3 + i) * KC
            kb.dma(sp, lng[:, o:o + KC], W["ln_g"][l, i].rearrange("(c p) -> p c", p=128), writes=[lng])
            kb.dma(sp, lnb[:, o:o + KC], W["ln_b"][l, i].rearrange("(c p) -> p c", p=128), writes=[lnb])
        kb.dma(sp, pleb[:, l * KC:(l + 1) * KC], W["ple_b_gate"][l].rearrange("(c p) -> p c", p=128), writes=[pleb])
    kb.op(dve, lambda h: h.tensor_scalar(out=lngs[:], in0=lng[:], scalar1=ALPHA, scalar2=None, op0=ALU.mult),
          reads=[lng], writes=[lngs])
    kb.op(dve, lambda h: h.tensor_scalar(out=lnbs[:], in0=lnb[:], scalar1=ALPHA, scalar2=None, op0=ALU.mult),
          reads=[lnb], writes=[lnbs])
    nbf = sb("nbf", [8, DEPTH], F32)
    if "nbf" not in SKIP:
        kb.dma(sp, nbf[:], W["fox_b_f"].rearrange("l h -> h l"), writes=[nbf])
    kb.op(dve, lambda h: h.tensor_scalar(out=nbf[:], in0=nbf[:], scalar1=-1.0, scalar2=None, op0=ALU.mult),
          reads=[nbf], writes=[nbf])
    peT = sb("peT", [64, DEPTH * 2 * 32], F32)
    for l in range(DEPTH if "pet" not in SKIP else 0):
        kb.dma(sp, peT[:, (l * 2) * 32:(l * 2 + 1) * 32], W["nsa_pos_k"][l].rearrange("a d -> d a"), writes=[peT])
        kb.dma(sp, peT[:, (l * 2 + 1) * 32:(l * 2 + 2) * 32], W["nsa_pos_v"][l].rearrange("a d -> d a"), writes=[peT])
    dlam = sb("dlam", [64, DEPTH * 128], F32)
    for l in range(DEPTH if "dlam" not in SKIP else 0):
        src = W["diff_lambda"][l].rearrange("a b -> (a b)")
        kb.dma(sp, dlam[:, l * 128:(l + 1) * 128], mkap(src, 0, [(1, 128)]) if False else
               bass.AP(tensor=src.tensor, offset=src.offset, ap=[[0, 64], [1, 128]]), writes=[dlam])
    dprod = sb("dprod", [64, DEPTH * 64], F32)
    nlam = sb("nlam", [64, DEPTH], F32)
    dsum = sb("dsum", [64, DEPTH * 2], F32)
    dgs = sb("dgs", [64, DEPTH], F32)
    if "dlam" not in SKIP:
        kb.dma(sp, dgs[:], W["diff_subln_g"].rearrange("l d -> d l"), writes=[dgs])
    for l in range(DEPTH if "dlam" not in SKIP else 0):
        linit = 0.8 - 0.6 * math.exp(-0.3 * l)
        for a in range(2):
            o = l * 128 + a * 64
            kb.op(dve, lambda h: h.tensor_tensor(out=dprod[:, l * 64 + a * 32:l * 64 + (a + 1) * 32],
                                                 in0=dlam[:, o:o + 32], in1=dlam[:, o + 32:o + 64], op=ALU.mult),
                  reads=[dlam], writes=[dprod])
            kb.op(dve, lambda h: h.reduce_sum(out=dsum[:, l * 2 + a:l * 2 + a + 1],
                                              in_=dprod[:, l * 64 + a * 32:l * 64 + (a + 1) * 32], axis=AX.X),
                  reads=[dprod], writes=[dsum])
        kb.op(act, lambda h: h.activation(out=dsum[:, l * 2:l * 2 + 2], in_=dsum[:, l * 2:l * 2 + 2], func=AF.Exp),
              reads=[dsum], writes=[dsum])
        kb.op(dve, lambda h: h.tensor_tensor(out=nlam[:, l:l + 1], in0=dsum[:, l * 2 + 1:l * 2 + 2],
                                             in1=dsum[:, l * 2:l * 2 + 1], op=ALU.subtract),
              reads=[dsum], writes=[nlam])
        kb.op(dve, lambda h: h.tensor_scalar(out=nlam[:, l:l + 1], in0=nlam[:, l:l + 1], scalar1=-linit, scalar2=None,
                                             op0=ALU.add), reads=[nlam], writes=[nlam])
        kb.op(dve, lambda h: h.tensor_scalar(out=dgs[:, l:l + 1], in0=dgs[:, l:l + 1], scalar1=1.0 - linit,
                                             scalar2=None, op0=ALU.mult), reads=[dgs], writes=[dgs])

    x32 = [None]

    def X32():
        return x32[0]

    def tile_sl(t):
        return slice(t * 512, (t + 1) * 512)

    def load_x(si):
        x32[0] = big.view("x32", 0, (KC, S), F32)
        X = x32[0]
        for t in range(NT):
            stage = vreg.view("xstage", 0, (4, 1024), F32)
            for i in range(4):
                kb.dma(sp, stage[:, i, :], x_d[si, (4 * t + i) * 128:(4 * t + i + 1) * 128, :], writes=[stage])
            for c in range(KC):
                pst = rotA()
                for i in range(4):
                    kb.op(pe, lambda h: h.transpose(pst[:, i * 128:(i + 1) * 128], stage[:, i, c * 128:(c + 1) * 128],
                                                    cF["ident"][:]), reads=[stage, cF["ident"]], writes=[pst])
                kb.op(act, lambda h: h.mul(X[:, c, tile_sl(t)], pst[:], ALPHA), reads=[pst], writes=[X])
                kb.op(act, lambda h: h.copy(xb[:, c, tile_sl(t)], pst[:]), reads=[pst], writes=[xb])

    def wview(w2d):
        return w2d.rearrange("(k p) c -> p k c", p=128)

    def dst3(n0, k, c):
        return lambda t, o: t[:, o + n0:o + n0 + k * c].rearrange("p (k c) -> p k c", c=c)

    def ffn_chunks(l, which):
        wg = wview(W[f"ffn{which}_w_gate"][l])
        wu = wview(W[f"ffn{which}_w_up"][l])
        wd = W[f"ffn{which}_w_down"][l]
        ids = []
        for f in range(NF):
            fw = 128 if f < NF - 1 else DFF - 128 * (NF - 1)
            dmas = [(dst3(0, KC, fw), wg[:, :, f * 128:f * 128 + fw]),
                    (dst3(1024, KC, fw), wu[:, :, f * 128:f * 128 + fw]),
                    ((lambda t, o, fw=fw: t[0:fw, o + 2048:o + 3072]), wd[f * 128:f * 128 + fw, :])]
            ids.append(ring.add(1, dmas))
        return ids

    def ffn(ids):
        X = X32()
        steps = [(g, t) for g in range(NF // 2) for t in range(NT)]
        acq = {}

        def gu_parts(g, t):
            fs = (2 * g, 2 * g + 1)
            if g not in acq:
                acq[g] = [ring.acquire(ids[f]) for f in fs]
            Hs = []
            parts = []
            for fi, f in enumerate(fs):
                fw = 128 if f < NF - 1 else DFF - 128 * (NF - 1)
                tt, o, sbufs = acq[g][fi]
                st = {}

                def mm(key, off, k0, k1, fw=fw, tt=tt, o=o, sbufs=sbufs, st=st):
                    if k0 == 0:
                        st[key] = rotA()
                    ps = st[key]
                    for k in range(k0, k1):
                        kb.op(pe, lambda h: h.matmul(ps[0:fw, :], lhsT=tt[:, o + off + k * fw:o + off + (k + 1) * fw],
                                                     rhs=xb[:, k, tile_sl(t)], start=(k == 0), stop=(k == KC - 1)),
                              reads=[xb] + sbufs, writes=[ps])

                def fin(fw=fw, tt=tt, o=o, sbufs=sbufs, st=st):
                    psg, psu = st["g"], st["u"]
                    sg = tf()
                    kb.op(act, lambda h: h.activation(out=sg[0:fw, :], in_=psg[0:fw, :], func=AF.Silu),
                          reads=[psg], writes=[sg])
                    Hb = nextH()
                    kb.op(dve, lambda h: h.tensor_tensor(out=Hb[0:fw, :], in0=sg[0:fw, :], in1=psu[0:fw, :], op=ALU.mult),
                          reads=[sg, psu], writes=[Hb])
                    Hs.append((Hb, fw, tt, o, sbufs))

                parts.append(lambda mm=mm: mm("g", 0, 0, 4))
                parts.append(lambda mm=mm: mm("g", 0, 4, 8))
                parts.append(lambda mm=mm: mm("u", 1024, 0, 4))
                parts.append(lambda mm=mm, fin=fin: (mm("u", 1024, 4, 8), fin()))
            return parts, Hs

        def down_m(t, Hs, m):
            psd = rotB()
            for i, (Hb, fw, tt, o, sbufs) in enumerate(Hs):
                kb.op(pe, lambda h: h.matmul(psd[:, :], lhsT=tt[0:fw, o + 2048 + m * 128:o + 2048 + (m + 1) * 128],
                                             rhs=Hb[0:fw, :], start=(i == 0), stop=(i == len(Hs) - 1)),
                      reads=[Hb] + sbufs, writes=[psd])
            kb.op(dve, lambda h: h.scalar_tensor_tensor(out=X[:, m, tile_sl(t)], in0=psd[:, :], scalar=0.5,
                                                        in1=X[:, m, tile_sl(t)], op0=ALU.mult, op1=ALU.add),
                  reads=[psd, X], writes=[X])

        parts, Hs = gu_parts(*steps[0])
        for p_ in parts:
            p_()
        for i, (g, t) in enumerate(steps):
            if i + 1 < len(steps):
                nparts, nHs = gu_parts(*steps[i + 1])
            else:
                nparts, nHs = [], []
            for m in range(KC):
                down_m(t, Hs, m)
                for _ in range(2):
                    if nparts:
                        nparts.pop(0)()
            while nparts:
                nparts.pop(0)()
            if t == NT - 1:
                for f in (2 * g, 2 * g + 1):
                    ring.release(ids[f])
            Hs = nHs

    def ln_tile(Y, ysl, l, idx, scaled, xb_sl):
        yb = vreg.view("lnyb", 0, (KC, 512), BF16)
        ysq = vreg.view("lnysq", 8192, (KC, 512), BF16)
        kb.op(act, lambda h: h.activation(out=ysq[:], in_=Y[:, :, ysl], func=AF.Square), reads=[Y], writes=[ysq])
        kb.op(dve, lambda h: h.tensor_copy(out=yb[:], in_=Y[:, :, ysl]), reads=[Y], writes=[yb])
        ps1 = rotA()
        ps2 = rotA()
        for c in range(KC):
            kb.op(pe, lambda h: h.matmul(ps1[:, :], lhsT=oD_b[:], rhs=yb[:, c, :], start=(c == 0), stop=(c == KC - 1)),
                  reads=[yb, oD_b], writes=[ps1])
        for c in range(KC):
            kb.op(pe, lambda h: h.matmul(ps2[:, :], lhsT=oD_b[:], rhs=ysq[:, c, :], start=(c == 0), stop=(c == KC - 1)),
                  reads=[ysq, oD_b], writes=[ps2])
        msq = tf()
        kb.op(act, lambda h: h.activation(out=msq[:], in_=ps1[:], func=AF.Square), reads=[ps1], writes=[msq])
        var = tf()
        kb.op(dve, lambda h: h.tensor_tensor(out=var[:], in0=ps2[:], in1=msq[:], op=ALU.subtract), reads=[ps2, msq], writes=[var])
        kb.op(act, lambda h: h.activation(out=var[:], in_=var[:], func=AF.Ln, bias=eps_t[:, 0:1]), reads=[var, eps_t], writes=[var])
        kb.op(act, lambda h: h.activation(out=ps2[:], in_=var[:], func=AF.Exp, scale=-0.5), reads=[var], writes=[ps2])
        mb = mkap(ps1[:], 0, [(0, KC), (1, 512)])
        rb = mkap(ps2[:], 0, [(0, KC), (1, 512)])
        kb.op(dve, lambda h: h.tensor_tensor(out=Y[:, :, ysl], in0=Y[:, :, ysl], in1=mb, op=ALU.subtract),
              reads=[Y, ps1], writes=[Y])
        kb.op(dve, lambda h: h.tensor_tensor(out=Y[:, :, ysl], in0=Y[:, :, ysl], in1=rb, op=ALU.mult),
              reads=[Y, ps2], writes=[Y])
        o = (l * 3 + idx) * KC
        gs, bs = (lngs, lnbs) if scaled else (lng, lnb)
        for c in range(KC):
            kb.op(act, lambda h: h.activation(out=Y[:, c, ysl], in_=Y[:, c, ysl], func=AF.Identity,
                                              scale=gs[:, o + c:o + c + 1], bias=bs[:, o + c:o + c + 1]),
                  reads=[Y, gs, bs], writes=[Y])
        kb.op(dve, lambda h: h.tensor_scalar(out=xb[:, :, xb_sl], in0=Y[:, :, ysl], scalar1=(1.0 / ALPHA) if scaled else 1.0,
                                             scalar2=None, op0=ALU.mult), reads=[Y], writes=[xb])

    def layernorm(l, idx, scaled):
        for t in range(NT):
            ln_tile(X32(), tile_sl(t), l, idx, scaled, tile_sl(t))

    def ple(l, si, last, nxt=None):
        X = X32()
        wg = wview(W["ple_w_gate"][l])
        wp = W["ple_w_proj"][l].rearrange("(k p) c -> p k c", p=128)
        ids = []
        for m in range(KC):
            dmas = [(dst3(0, KC, 128), wg[:, :, m * 128:(m + 1) * 128]),
                    (dst3(1024, 2, 128), wp[:, :, m * 128:(m + 1) * 128])]
            ids.append(ring.add(1, dmas))
        if nxt is not None:
            MIXER["ffn1_ids"] = ffn_chunks(nxt, 1)
        pT = vreg.view("pT", 0, (2, S), BF16)
        for t in range(NT):
            stage = vreg.view(f"pstage", 8192, (4, PLE), F32)
            for i in range(4):
                kb.dma(sp, stage[:, i, :], p_d[l, si, (4 * t + i) * 128:(4 * t + i + 1) * 128, :], writes=[stage])
            for j in range(2):
                pst = rotA()
                for i in range(4):
                    kb.op(pe, lambda h: h.transpose(pst[:, i * 128:(i + 1) * 128], stage[:, i, j * 128:(j + 1) * 128],
                                                    cF["ident"][:]), reads=[stage, cF["ident"]], writes=[pst])
                kb.op(dve, lambda h: h.tensor_copy(out=pT[:, j, tile_sl(t)], in_=pst[:]), reads=[pst], writes=[pT])
        for m in range(KC):
            tt, o, sbufs = ring.acquire(ids[m])
            for t in range(NT):
                ps1 = rotA()
                ps2 = rotA()
                for k in range(KC):
                    kb.op(pe, lambda h: h.matmul(ps1[:, :], lhsT=tt[:, o + k * 128:o + (k + 1) * 128],
                                                 rhs=xb[:, k, tile_sl(t)], start=(k == 0), stop=(k == KC - 1)),
                          reads=[xb] + sbufs, writes=[ps1])
                for j in range(2):
                    kb.op(pe, lambda h: h.matmul(ps2[:, :], lhsT=tt[:, o + 1024 + j * 128:o + 1024 + (j + 1) * 128],
                                                 rhs=pT[:, j, tile_sl(t)], start=(j == 0), stop=(j == 1)),
                          reads=[pT] + sbufs, writes=[ps2])
                sg = tf()
                kb.op(act, lambda h: h.activation(out=sg[:], in_=ps1[:], func=AF.Sigmoid,
                                                  bias=pleb[:, l * KC + m:l * KC + m + 1]),
                      reads=[ps1, pleb], writes=[sg])
                kb.op(dve, lambda h: h.tensor_tensor(out=sg[:], in0=sg[:], in1=ps2[:], op=ALU.mult),
                      reads=[sg, ps2], writes=[sg])
                kb.op(dve, lambda h: h.tensor_tensor(out=X[:, m, tile_sl(t)], in0=X[:, m, tile_sl(t)], in1=sg[:],
                                                     op=ALU.add), reads=[sg, X], writes=[X])
            ring.release(ids[m])
        if not last:
            for c in range(KC):
                kb.op(act, lambda h: h.copy(xb[:, c, :], X[:, c, :]), reads=[X], writes=[xb])
                kb.op(dve, lambda h: h.tensor_scalar(out=X[:, c, :], in0=X[:, c, :], scalar1=ALPHA, scalar2=None,
                                                     op0=ALU.mult), reads=[X], writes=[X])

    def store_out(si, scale=1.0):
        X = X32()
        for b in range(NB):
            stage = vreg.view("ostage", (b % 2) * 4096, (1024,), F32)
            for half in range(2):
                pst = rotA()
                for cc in range(4):
                    c = half * 4 + cc
                    kb.op(pe, lambda h: h.transpose(pst[:, cc * 128:(cc + 1) * 128], X[:, c, b * 128:(b + 1) * 128],
                                                    cF["ident"][:]), reads=[X, cF["ident"]], writes=[pst])
                kb.op(act, lambda h: h.mul(stage[:, half * 512:(half + 1) * 512], pst[:], scale),
                      reads=[pst], writes=[stage])
            tok = kb.dma(sp, out_d[si, b * 128:(b + 1) * 128, :], stage[:], reads=[stage])
            kb.out_toks.append(tok)

    MIXER = {}
    xspB = Buf(None, "xsp")
    ones_f = sb("ones_f", [128, 8], F32)
    kb.op(dve, lambda h: h.memset(ones_f[:], 1.0), writes=[ones_f])
    csT = sb("csT", [128, 128], F32)
    def causal_struct(j):
        res = []
        for kt in range(4 * j + 4):
            if kt < 4 * j:
                res.append((kt, 0, 512, []))
            else:
                i = kt - 4 * j
                res.append((kt, 128 * i, 512, [(i, "tri")]))
        return res

    def window_struct(j):
        res = []
        for m in range(8):
            kt = 4 * j - 4 + m
            if kt < 0:
                continue
            lo, hi = max(0, m - 4), min(3, m)
            masks = []
            if m <= 3:
                masks.append((m, "anti"))
            if m >= 4:
                masks.append((m - 4, "tri"))
            res.append((kt, 128 * lo, 128 * (hi + 1), masks))
        return res

    LOOK = 2

    def attend(j, struct, QT, qbufs, KT_fn, kbufs, nk_fn, vaug_fn, vbufs, scale, bias_fn=None, bbufs=(), hook=None, bg=None, bgs=None, smask=None):
        acc = rotB()
        lst = struct(j)
        n = len(lst)
        pss_l = [None] * n

        def emit_s(idx):
            kt, c0, c1, masks = lst[idx]
            nk = nk_fn(kt)
            pss = rotS()
            pss_l[idx] = pss
            sm = smask(kt, j, c0, c1) if smask is not None else None
            kb.op(pe, lambda h: h.matmul(pss[0:nk, c0:c1], lhsT=KT_fn(kt), rhs=QT[:, j * 512 + c0:j * 512 + c1],
                                         start=True, stop=(len(masks) == 0 and sm is None)),
                  reads=list(qbufs) + list(kbufs), writes=[pss])
            if sm is not None:
                sm_l, sm_r, sm_bufs = sm
                kb.op(pe, lambda h: h.matmul(pss[0:nk, c0:c1], lhsT=sm_l, rhs=sm_r, start=False, stop=(len(masks) == 0)),
                      reads=list(sm_bufs), writes=[pss])
            for mi, (i, mname) in enumerate(masks):
                mk = cF[mname]
                kb.op(pe, lambda h: h.matmul(pss[0:nk, i * 128:(i + 1) * 128], lhsT=cF["identb"][:, 0:nk], rhs=mk[:, :],
                                             start=False, stop=(mi == len(masks) - 1)), reads=[mk, cF["identb"]], writes=[pss])

        for idx in range(min(LOOK, n)):
            emit_s(idx)
        for idx, (kt, c0, c1, masks) in enumerate(lst):
            nk = nk_fn(kt)
            pss = pss_l[idx]
            P = nextP()
            if bias_fn is not None:
                kb.op(act, lambda h: h.activation(out=P[0:nk, c0:c1], in_=pss[0:nk, c0:c1], func=AF.Exp, scale=scale,
                                                  bias=bias_fn(kt)), reads=[pss] + list(bbufs), writes=[P])
            else:
                kb.op(act, lambda h: h.activation(out=P[0:nk, c0:c1], in_=pss[0:nk, c0:c1], func=AF.Exp, scale=scale),
                      reads=[pss], writes=[P])
            if idx + LOOK < n:
                emit_s(idx + LOOK)
            if hook is not None:
                hook(kt, j, c0, c1, P, nk)
            kb.op(pe, lambda h: h.matmul(acc[:, c0:c1], lhsT=vaug_fn(kt), rhs=P[0:nk, c0:c1], start=(idx == 0),
                                         stop=(idx == n - 1), skip_group_check=True), reads=[P] + list(vbufs), writes=[acc])
            if bg:
                if bgs is None:
                    bg.pop(0)()
                else:
                    bgs["n"] += 1
                    if bgs["n"] % bgs["every"] == 0:
                        bg.pop(0)()
        return acc

    def recip_act(ap, bufs):
        kb.op(act, lambda h: h.activation(out=ap, in_=ap, func=AF.Ln), reads=list(bufs), writes=list(bufs))
        kb.op(act, lambda h: h.activation(out=ap, in_=ap, func=AF.Exp, scale=-1.0), reads=list(bufs), writes=list(bufs))

    def norm_out(acc, dst_ap, dst_bufs, mul_ap=None, mul_bufs=(), use_act=False):
        r = tf()
        if use_act:
            kb.op(act, lambda h: h.activation(out=r[64:128, :], in_=acc[0:64, :], func=AF.Ln), reads=[acc], writes=[r])
            kb.op(act, lambda h: h.activation(out=r[64:128, :], in_=r[64:128, :], func=AF.Exp, scale=-1.0), reads=[r], writes=[r])
        else:
            kb.op(dve, lambda h: h.reciprocal(out=r[64:128, :], in_=acc[0:64, :]), reads=[acc], writes=[r])
        if mul_ap is not None:
            kb.op(dve, lambda h: h.tensor_tensor(out=r[64:128, :], in0=r[64:128, :], in1=mul_ap, op=ALU.mult),
                  reads=[r] + list(mul_bufs), writes=[r])
        kb.op(dve, lambda h: h.tensor_tensor(out=dst_ap, in0=acc[64:128, :], in1=r[64:128, :], op=ALU.mult),
              reads=[acc, r], writes=list(dst_bufs))

    def proj_fm(tt, o, ncol, kstride, sbufs, t, rows):
        ps = rotX()
        for k in range(KC):
            kb.op(pe, lambda h: h.matmul(ps[0:rows, :], lhsT=tt[:, o + k * kstride:o + k * kstride + rows],
                                         rhs=xb[:, k, tile_sl(t)], start=(k == 0), stop=(k == KC - 1)),
                  reads=[xb] + sbufs, writes=[ps])
        return ps

    def rope_a(ps, rows):
        t1 = rtf()
        kb.op(act, lambda h: h.copy(t1[0:rows, :], ps[0:rows, :]), reads=[ps], writes=[t1])
        return t1

    def rope_b(t1, rows, t, dst_ap, dst_bufs, d, dsts=None):
        cosT, sinT, rt = (cF["cos64"], cF["sin64"], cF["rt64"]) if d == 64 else (cF["cos32"], cF["sin32"], cF["rt32"])
        p2 = rotX()
        kb.op(pe, lambda h: h.matmul(p2[0:rows, :], lhsT=rt[0:rows, 0:rows], rhs=t1[0:rows, :], start=True, stop=True),
              reads=[t1, rt], writes=[p2])
        t2 = rtf()
        kb.op(dve, lambda h: h.tensor_tensor(out=t2[0:rows, :], in0=p2[0:rows, :], in1=sinT[0:rows, tile_sl(t)], op=ALU.mult),
              reads=[p2, sinT], writes=[t2])
        kb.op(dve, lambda h: h.tensor_tensor(out=t1[0:rows, :], in0=t1[0:rows, :], in1=cosT[0:rows, tile_sl(t)], op=ALU.mult),
              reads=[t1, cosT], writes=[t1])
        if dsts is None:
            dsts = [(0, rows, dst_ap, dst_bufs)]
        for (r0, r1, dap, dbufs) in dsts:
            kb.op(dve, lambda h: h.tensor_tensor(out=dap, in0=t1[r0:r1, :], in1=t2[r0:r1, :], op=ALU.add),
                  reads=[t1, t2], writes=list(dbufs))

    def rope_evac(ps, rows, t, dst_ap, dst_bufs, d):
        t1 = rope_a(ps, rows)
        rope_b(t1, rows, t, dst_ap, dst_bufs, d)

    def vproj(col0, vaug):
        w = 256
        cid = ring.add(1, [(dst3(0, KC, w), win_v[0][:, :, col0:col0 + w])])
        tt, o, sbufs = ring.acquire(cid)
        for blk in range(NB):
            ps = rotA()
            for k in range(KC):
                kb.op(pe, lambda h: h.matmul(ps[:, 0:w], lhsT=xb[:, k, blk * 128:(blk + 1) * 128],
                                             rhs=tt[:, o + k * w:o + (k + 1) * w], start=(k == 0), stop=(k == KC - 1)),
                      reads=[xb] + sbufs, writes=[ps])
            kb.op(act, lambda h: h.copy(vaug[:, blk, :, 64:128], ps[:, 0:w].rearrange("p (h d) -> p h d", d=64)),
                  reads=[ps], writes=[vaug])
        ring.release(cid)

    win_v = [None]

    def vaug_ap(vaug, kt, hl):
        return vaug[:, kt, hl, :]

    def mixer(l, si):
        global_win = wview(W["w_in"][l])
        win_v[0] = global_win
        X = X32()
        xsp3 = xsp_d.rearrange("p (c s) -> p c s", s=S)
        for c in range(KC):
            kb.dma(sp, xsp3[:, c, :], X[:, c, :], reads=[X], writes=[xspB])
        concat = big.view("concat", 0, (KC, S), BF16)
        QO = 32768
        win = global_win

        spb = big.view("fsp", QO + 16384, (S,), F32)
        ncs8 = big.view("fncs8", QO + 24576, (S,), BF16)
        cid = ring.add(1, [(dst3(0, KC, 8), win[:, :, C_FF:C_FF + 8])])
        tt, o, sbufs = ring.acquire(cid)
        for t in range(NT):
            ps = proj_fm(tt, o, 8, 8, sbufs, t, 8)
            e1 = tf()
            kb.op(act, lambda h: h.activation(out=e1[0:8, :], in_=ps[0:8, :], func=AF.Exp, scale=-1.0, bias=nbf[:, l:l + 1]),
                  reads=[ps, nbf], writes=[e1])
            kb.op(act, lambda h: h.activation(out=spb[0:8, tile_sl(t)], in_=e1[0:8, :], func=AF.Ln, bias=1.0),
                  reads=[e1], writes=[spb])
        ring.release(cid)
        kb.op(dve, lambda h: h.tensor_tensor_scan(out=spb[0:8, :], data0=mkap(ones_f[0:8, 0:1], 0, [(0, S)]),
                                                  data1=spb[0:8, :], initial=0.0, op0=ALU.mult, op1=ALU.add),
              reads=[spb, ones_f], writes=[spb])
        kb.op(dve, lambda h: h.tensor_scalar(out=ncs8[0:8, :], in0=spb[0:8, :], scalar1=-8.0, scalar2=None, op0=ALU.mult),
              reads=[spb], writes=[ncs8])
        pst = rotA()
        for blk in range(NB):
            kb.op(pe, lambda h: h.transpose(pst[:, blk * 8:(blk + 1) * 8], spb[0:8, blk * 128:(blk + 1) * 128],
                                            cF["ident"][0:8, 0:8]), reads=[spb, cF["ident"]], writes=[pst])
        kb.op(dve, lambda h: h.tensor_copy(out=csT[:, 0:128], in_=pst[:, 0:128]), reads=[pst], writes=[csT])
        fbuf = [(big.view("fqa0", QO, (S,), BF16), big.view("fka0", QO + 4096, (S,), BF16)),
                (big.view("fqa1", QO + 8192, (S,), BF16), big.view("fka1", QO + 12288, (S,), BF16))]

        def fox_jobs(hh):
            qa, ka = fbuf[hh % 2]
            st = {}

            def j_acq():
                qk3 = lambda t_, o_: t_[:, o_:o_ + 1024].rearrange("p (k c) -> p k c", c=128)
                cid = ring.add(1, [((lambda t_, o_: qk3(t_, o_)[:, :, 0:64]), win[:, :, C_FQ + hh * 64:C_FQ + (hh + 1) * 64]),
                                   ((lambda t_, o_: qk3(t_, o_)[:, :, 64:128]), win[:, :, C_FK + hh * 64:C_FK + (hh + 1) * 64])])
                st["cid"] = cid
                st["acq"] = ring.acquire(cid)
                kb.dma(sp, qa[64:65, :], ncs8[hh:hh + 1, :], reads=[ncs8], writes=[qa])
                kb.op(dve, lambda h: h.memset(ka[64:65, :], 1.0), writes=[ka])

            def j_proj(t):
                def f():
                    tt, o, sbufs = st["acq"]
                    ps = proj_fm(tt, o, 128, 128, sbufs, t, 128)
                    kb.op(dve, lambda h: h.tensor_copy(out=qa[0:64, tile_sl(t)], in_=ps[0:64, :]), reads=[ps], writes=[qa])
                    kb.op(dve, lambda h: h.tensor_copy(out=ka[0:64, tile_sl(t)], in_=ps[64:128, :]), reads=[ps], writes=[ka])
                    if t == NT - 1:
                        ring.release(st["cid"])
                return f

            return [j_acq] + [j_proj(t) for t in range(NT)]

        for job in fox_jobs(0):
            job()
        attn["on"] = True
        for hh in range(8):
            if hh % 4 == 0:
                vaug = vreg.view("vaug", 0, (NB, 4, 128), BF16)
                kb.op(dve, lambda h: h.memset(vaug[:, :, :, 0:64], 1.0), writes=[vaug])
                vproj(C_FV + hh * 64, vaug)
            qa, ka = fbuf[hh % 2]
            bg = fox_jobs(hh + 1) if hh + 1 < 8 else []
            bgs = {"n": 0, "every": 7}
            for j in range(NT):
                acc = attend(j, causal_struct, qa[0:65, :], [qa], lambda kt: ka[0:65, kt * 128:(kt + 1) * 128], [ka],
                             lambda kt: 128, lambda kt: vaug_ap(vaug, kt, hh % 4), [vaug], 0.125,
                             bias_fn=lambda kt: csT[:, kt * 8 + hh:kt * 8 + hh + 1], bbufs=[csT], bg=bg, bgs=bgs)
                norm_out(acc, concat[(hh % 2) * 64:(hh % 2) * 64 + 64, 4 + hh // 2, tile_sl(j)], [concat])
            while bg:
                bg.pop(0)()

        mark("fox")
        vaug = vreg.view("vaug", 0, (NB, 4, 128), BF16)
        kb.op(dve, lambda h: h.memset(vaug[:, :, :, 0:64], 1.0), writes=[vaug])
        vproj(C_DV, vaug)
        dbuf = []
        for i_ in range(2):
            b0 = QO + i_ * 12288
            qd_ = big.view("dq%d" % i_, b0, (S,), BF16)
            kz0_ = big.view("dkz0%d" % i_, b0 + 4096, (S,), BF16)
            kz1_ = big.view("dkz1%d" % i_, b0 + 8192, (S,), BF16)
            kb.op(dve, lambda h: h.memset(qd_[64:128, :], 0.0), writes=[qd_])
            kb.op(dve, lambda h: h.memset(kz0_[32:64, :], 0.0), writes=[kz0_])
            kb.op(dve, lambda h: h.memset(kz0_[64:128, :], 0.0), writes=[kz0_])
            kb.op(dve, lambda h: h.memset(kz1_[64:128, :], 0.0), writes=[kz1_])
            dbuf.append((qd_, kz0_, kz1_))

        def diff_jobs(hh):
            qd, kz0, kz1 = dbuf[hh % 2]
            kd = kz1
            st = {}

            def j_acq():
                qk3 = lambda t_, o_: t_[:, o_:o_ + 1024].rearrange("p (k c) -> p k c", c=128)
                cid = ring.add(1, [((lambda t_, o_: qk3(t_, o_)[:, :, 0:64]), win[:, :, C_DQ + hh * 64:C_DQ + (hh + 1) * 64]),
                                   ((lambda t_, o_: qk3(t_, o_)[:, :, 64:128]), win[:, :, C_DK + hh * 64:C_DK + (hh + 1) * 64])])
                st["cid"] = cid
                st["acq"] = ring.acquire(cid)

            def j_proj(t):
                def fa():
                    tt, o, sbufs = st["acq"]
                    ps = proj_fm(tt, o, 128, 128, sbufs, t, 128)
                    st["t1"] = rope_a(ps, 128)
                    if t == NT - 1:
                        ring.release(st["cid"])

                def fb():
                    rope_b(st["t1"], 128, t, None, None, 32,
                           dsts=[(0, 64, qd[0:64, tile_sl(t)], [qd]), (64, 128, kd[0:64, tile_sl(t)], [kd])])
                return [fa, fb]

            def j_split():
                kb.op(dve, lambda h: h.tensor_copy(out=kz0[0:32, :], in_=kz1[0:32, :]), reads=[kz1], writes=[kz0])
                kb.op(dve, lambda h: h.memset(kz1[0:32, :], 0.0), writes=[kz1])

            jobs = [j_acq]
            for t in range(NT):
                jobs += j_proj(t)
            jobs.append(j_split)
            return jobs

        for job in diff_jobs(0):
            job()
        for hh in range(4):
            qd, kz0, kz1 = dbuf[hh % 2]
            kzs = (kz0, kz1)
            bg = diff_jobs(hh + 1) if hh + 1 < 4 else []
            bgs = {"n": 0, "every": 7}
            for j in range(NT):
                os_ = []
                for c in range(2):
                    kz = kzs[c]
                    acc = attend(j, causal_struct, qd[0:128, :], [qd],
                                 lambda kt: kz[0:128, kt * 128:(kt + 1) * 128], [kz], lambda kt: 128,
                                 lambda kt: vaug_ap(vaug, kt, hh), [vaug], 32 ** -0.5, bg=bg, bgs=bgs)
                    oc = tf()
                    norm_out(acc, oc[0:64, :], [oc])
                    os_.append(oc)
                o0, o1 = os_
                kb.op(dve, lambda h: h.scalar_tensor_tensor(out=o0[0:64, :], in0=o1[0:64, :], scalar=nlam[0:64, l:l + 1],
                                                            in1=o0[0:64, :], op0=ALU.mult, op1=ALU.add),
                      reads=[o0, o1, nlam], writes=[o0])
                sq = nextP()
                kb.op(dve, lambda h: h.tensor_tensor(out=sq[0:64, :], in0=o0[0:64, :], in1=o0[0:64, :], op=ALU.mult),
                      reads=[o0], writes=[sq])
                pm = rotX()
                kb.op(pe, lambda h: h.matmul(pm[0:64, :], lhsT=o64_b[0:64, 0:64], rhs=sq[0:64, :], start=True, stop=True),
                      reads=[sq, o64_b], writes=[pm])
                rs = o1
                kb.op(act, lambda h: h.activation(out=rs[0:64, :], in_=pm[0:64, :], func=AF.Ln, bias=eps_t[0:64, 0:1]),
                      reads=[pm, eps_t], writes=[rs])
                kb.op(act, lambda h: h.activation(out=rs[0:64, :], in_=rs[0:64, :], func=AF.Exp, scale=-0.5), reads=[rs], writes=[rs])
                kb.op(dve, lambda h: h.scalar_tensor_tensor(out=concat[(hh % 2) * 64:(hh % 2) * 64 + 64, 2 + hh // 2, tile_sl(j)],
                                                            in0=o0[0:64, :], scalar=dgs[0:64, l:l + 1], in1=rs[0:64, :],
                                                            op0=ALU.mult, op1=ALU.mult),
                      reads=[o0, rs, dgs], writes=[concat])
            while bg:
                bg.pop(0)()

        mark("diff")
        attn["on"] = False
        nsa(l, concat, win, QO)
        attn["on"] = False
        attn["imp"] = False
        mark("nsa")

        if dbg and l == 0 and si == 0:
            d3 = dbg_d["cat"].rearrange("p (c s) -> p c s", s=S)
            for c in range(KC):
                kb.out_toks.append(kb.dma(pool, d3[:, c, :], concat[:, c, :], reads=[concat]))
        wo = wview(W["w_out"][l])
        Ys = {}

        def wo_mm(t):
            ids = [ring.add(1, [(dst3(0, KC, 128), wo[:, :, m * 128:(m + 1) * 128])]) for m in range(KC)]
            if t == NT - 1:
                MIXER["ffn2_ids"] = ffn_chunks(l, 2)
            Y = big.view("ytile%d" % (t % 2), QO + (t % 2) * 16384, (KC, 512), F32)
            Ys[t] = Y
            for c in range(KC):
                kb.dma(sp, Y[:, c, :], xsp3[:, c, tile_sl(t)], reads=[xspB], writes=[Y])
            for m in range(KC):
                tt, o, sbufs = ring.acquire(ids[m])
                ps = rotB()
                for c in range(KC):
                    kb.op(pe, lambda h: h.matmul(ps[:, :], lhsT=tt[:, o + c * 128:o + (c + 1) * 128],
                                                 rhs=concat[:, c, tile_sl(t)], start=(c == 0), stop=(c == KC - 1)),
                          reads=[concat] + sbufs, writes=[ps])
                ring.release(ids[m])
                kb.op(dve, lambda h: h.tensor_tensor(out=Y[:, m, :], in0=Y[:, m, :], in1=ps[:, :], op=ALU.add),
                      reads=[Y, ps], writes=[Y])

        wo_mm(0)
        for t in range(NT):
            if t + 1 < NT:
                wo_mm(t + 1)
            Y = Ys[t]
            ln_tile(Y, slice(0, 512), l, 1, True, tile_sl(t))
            for c in range(KC):
                kb.dma(sp, xsp3[:, c, tile_sl(t)], Y[:, c, :], reads=[Y], writes=[xspB])
        x32[0] = big.view("x32", 0, (KC, S), F32)
        Xn = x32[0]
        for c in range(KC):
            kb.dma(sp, Xn[:, c, :], xsp3[:, c, :], reads=[xspB], writes=[Xn])

    def rope_cols(ps, rows, w, cos_ap, sin_ap, dst_ap, dst_bufs, rt):
        t1 = rtf()
        kb.op(act, lambda h: h.copy(t1[0:rows, 0:w], ps[0:rows, 0:w]), reads=[ps], writes=[t1])
        p2 = rotX()
        kb.op(pe, lambda h: h.matmul(p2[0:rows, 0:w], lhsT=rt[0:rows, 0:rows], rhs=t1[0:rows, 0:w], start=True, stop=True),
              reads=[t1, rt], writes=[p2])
        t2 = rtf()
        kb.op(dve, lambda h: h.tensor_tensor(out=t2[0:rows, 0:w], in0=p2[0:rows, 0:w], in1=sin_ap, op=ALU.mult),
              reads=[p2, cF["sin64"]], writes=[t2])
        kb.op(dve, lambda h: h.tensor_tensor(out=t1[0:rows, 0:w], in0=t1[0:rows, 0:w], in1=cos_ap, op=ALU.mult),
              reads=[t1, cF["cos64"]], writes=[t1])
        kb.op(dve, lambda h: h.tensor_tensor(out=dst_ap, in0=t1[0:rows, 0:w], in1=t2[0:rows, 0:w], op=ALU.add),
              reads=[t1, t2], writes=list(dst_bufs))

    def nsa(l, concat, win, QO):
        qn = [big.view(f"nq{h_}", QO + h_ * 4096, (S,), BF16) for h_ in range(4)]
        kslc = big.view("nkslc", QO + 16384, (S,), BF16)
        kwin = big.view("nkwin", QO + 20480, (S,), BF16)
        gT = big.view("ngT", QO + 24576, (S,), BF16)
        selT = big.view("nselT", QO + 28672, (1024,), BF16)
        kcT = big.view("nkcT", QO + 30720, (128,), BF16)
        vca = big.view("nvca", QO + 30976, (64,), BF16)
        kcmp = vreg.view("nkcmp", 0, (S,), BF16)
        vcmp = vreg.view("nvcmp", 4096, (S,), BF16)
        blk = vreg.view("nblk", 8192, (32, 128), BF16)
        c64, s64, r64 = cF["cos64"], cF["sin64"], cF["rt64"]
        for b_ in qn + [kslc, kwin]:
            kb.op(dve, lambda h: h.memset(b_[64:128, :], 0.0), writes=[b_])
        kb.op(dve, lambda h: h.memset(selT[:, :], 0.0), writes=[selT])
        cid = ring.add(1, [(dst3(0, KC, 256), win[:, :, 0:256])])
        tt, o, sbufs = ring.acquire(cid)
        for p_ in range(2):
            for t in range(NT):
                ps = proj_fm(tt, o + p_ * 128, 128, 256, sbufs, t, 128)
                t1 = rope_a(ps, 128)
                rope_b(t1, 128, t, None, None, 64,
                       dsts=[(0, 64, qn[2 * p_][0:64, tile_sl(t)], [qn[2 * p_]]),
                             (64, 128, qn[2 * p_ + 1][0:64, tile_sl(t)], [qn[2 * p_ + 1]])])
        ring.release(cid)
        cid = ring.add(1, [(lambda t_, o_: t_[:, o_:o_ + 2048].rearrange("p (k c) -> p k c", c=256)[:, :, 0:64], win[:, :, C_KCMP:C_KCMP + 64]),
                           (lambda t_, o_: t_[:, o_:o_ + 2048].rearrange("p (k c) -> p k c", c=256)[:, :, 64:128], win[:, :, C_VCMP:C_VCMP + 64]),
                           (lambda t_, o_: t_[:, o_:o_ + 2048].rearrange("p (k c) -> p k c", c=256)[:, :, 128:192], win[:, :, C_KSLC:C_KSLC + 64]),
                           (lambda t_, o_: t_[:, o_:o_ + 2048].rearrange("p (k c) -> p k c", c=256)[:, :, 192:256], win[:, :, C_KWIN:C_KWIN + 64]),
                           (dst3(2048, KC, 12), win[:, :, C_NG:C_NG + 12])])
        tt, o, sbufs = ring.acquire(cid)
        for t in range(NT):
            ps = proj_fm(tt, o, 128, 256, sbufs, t, 128)
            kb.op(act, lambda h: h.copy(kcmp[0:64, tile_sl(t)], ps[0:64, :]), reads=[ps], writes=[kcmp])
            kb.op(act, lambda h: h.copy(vcmp[0:64, tile_sl(t)], ps[64:128, :]), reads=[ps], writes=[vcmp])
            ps = proj_fm(tt, o + 128, 128, 256, sbufs, t, 128)
            t1 = rope_a(ps, 128)
            rope_b(t1, 128, t, None, None, 64,
                   dsts=[(0, 64, kslc[0:64, tile_sl(t)], [kslc]), (64, 128, kwin[0:64, tile_sl(t)], [kwin])])
            ps = proj_fm(tt, o + 2048, 12, 12, sbufs, t, 12)
            kb.op(act, lambda h: h.activation(out=gT[0:12, tile_sl(t)], in_=ps[0:12, :], func=AF.Sigmoid),
                  reads=[ps], writes=[gT])
        ring.release(cid)
        mark("n_proj")
        for kv, (src, p1n, p2n) in enumerate([(kcmp, "nsa_phi_k1", "nsa_phi_k2"), (vcmp, "nsa_phi_v1", "nsa_phi_v2")]):
            for a in range(32):
                sap = mkap(src[0:64, a:a + 1], 0, [(16, 127)])
                kb.op(dve, lambda h: h.tensor_scalar(out=blk[0:64, a, 0:127], in0=sap,
                                                     scalar1=peT[0:64, (l * 2 + kv) * 32 + a:(l * 2 + kv) * 32 + a + 1],
                                                     scalar2=None, op0=ALU.add), reads=[src, peT], writes=[blk])
            p1src = W[p1n][l].rearrange("(a d) j -> d a j", d=64)
            c1 = ring.add(3, [((lambda t_, o_, q_=q_: t_[0:64, o_ + q_ * 2048:o_ + (q_ + 1) * 2048].rearrange("p (a j) -> p a j", j=256)),
                               p1src[:, q_ * 8:(q_ + 1) * 8, :]) for q_ in range(4)])
            c2 = ring.add(1, [(lambda t_, o_: t_[:, o_:o_ + 128].rearrange("p (c d) -> p c d", d=64),
                               W[p2n][l].rearrange("(c p) d -> p c d", p=128))])
            tt, o, sbufs = ring.acquire(c1)
            G = []
            for jc in range(2):
                ps = rotA()
                for a in range(32):
                    kb.op(pe, lambda h: h.matmul(ps[:, 0:127], lhsT=tt[0:64, o + a * 256 + jc * 128:o + a * 256 + (jc + 1) * 128],
                                                 rhs=blk[0:64, a, 0:127], start=(a == 0), stop=(a == 31)),
                          reads=[blk] + sbufs, writes=[ps])
                u = tf()
                kb.op(dve, lambda h: h.tensor_tensor(out=u[:, 0:127], in0=ps[:, 0:127], in1=ps[:, 0:127], op=ALU.mult) if False else
                      h.tensor_copy(out=u[:, 0:127], in_=ps[:, 0:127]), reads=[ps], writes=[u])
                v = tf()
                kb.op(dve, lambda h: h.tensor_tensor(out=v[:, 0:127], in0=u[:, 0:127], in1=u[:, 0:127], op=ALU.mult),
                      reads=[u], writes=[v])
                kb.op(dve, lambda h: h.tensor_scalar(out=v[:, 0:127], in0=v[:, 0:127], scalar1=0.044715, scalar2=1.0,
                                                     op0=ALU.mult, op1=ALU.add), reads=[v], writes=[v])
                kb.op(dve, lambda h: h.tensor_tensor(out=v[:, 0:127], in0=v[:, 0:127], in1=u[:, 0:127], op=ALU.mult),
                      reads=[u, v], writes=[v])
                kb.op(act, lambda h: h.activation(out=v[:, 0:127], in_=v[:, 0:127], func=AF.Tanh, scale=0.7978845608028654),
                      reads=[v], writes=[v])
                kb.op(dve, lambda h: h.tensor_scalar(out=v[:, 0:127], in0=v[:, 0:127], scalar1=1.0, scalar2=0.5,
                                                     op0=ALU.add, op1=ALU.mult), reads=[v], writes=[v])
                g_ = htiles[jc]
                kb.op(dve, lambda h: h.tensor_tensor(out=g_[:, 0:127], in0=v[:, 0:127], in1=u[:, 0:127], op=ALU.mult),
                      reads=[u, v], writes=[g_])
                G.append(g_)
            ring.release(c1)
            tt, o, sbufs = ring.acquire(c2)
            ps = rotA()
            if kv == 0:
                for jc in range(2):
                    kb.op(pe, lambda h: h.matmul(ps[0:64, 0:127], lhsT=tt[:, o + jc * 64:o + (jc + 1) * 64], rhs=G[jc][:, 0:127],
                                                 start=(jc == 0), stop=(jc == 1)), reads=G + sbufs, writes=[ps])
                rope_cols(ps, 64, 127, mkap(c64[0:64, 31:32], 0, [(16, 127)]), mkap(s64[0:64, 31:32], 0, [(16, 127)]),
                          kcT[0:64, 0:127], [kcT], r64)
            else:
                for jc in range(2):
                    kb.op(pe, lambda h: h.matmul(ps[0:127, 0:64], lhsT=G[jc][:, 0:127], rhs=tt[:, o + jc * 64:o + (jc + 1) * 64],
                                                 start=(jc == 0), stop=(jc == 1)), reads=G + sbufs, writes=[ps])
                kb.op(act, lambda h: h.copy(vca[0:127, 0:64], ps[0:127, 0:64]), reads=[ps], writes=[vca])
            ring.release(c2)
        mark("n_cmpr")
        vaug = vreg.view("vaug", 0, (NB, 2, 128), BF16)
        kb.op(dve, lambda h: h.memset(vaug[:, :, :, 0:64], 1.0), writes=[vaug])
        cid = ring.add(1, [(lambda t_, o_: t_[:, o_:o_ + 1024].rearrange("p (k c) -> p k c", c=128)[:, :, 0:64], win[:, :, C_VSLC:C_VSLC + 64]),
                           (lambda t_, o_: t_[:, o_:o_ + 1024].rearrange("p (k c) -> p k c", c=128)[:, :, 64:128], win[:, :, C_VWIN:C_VWIN + 64])])
        tt, o, sbufs = ring.acquire(cid)
        for b in range(NB):
            ps = rotA()
            for k in range(KC):
                kb.op(pe, lambda h: h.matmul(ps[:, 0:128], lhsT=xb[:, k, b * 128:(b + 1) * 128], rhs=tt[:, o + k * 128:o + (k + 1) * 128],
                                             start=(k == 0), stop=(k == KC - 1)), reads=[xb] + sbufs, writes=[ps])
            kb.op(act, lambda h: h.copy(vaug[:, b, :, 64:128], ps[:, 0:128].rearrange("p (h d) -> p h d", d=64)),
                  reads=[ps], writes=[vaug])
        ring.release(cid)

        def grep_(h_, br, t):
            pg = rotX()
            kb.op(pe, lambda h: h.matmul(pg[64:128, :], lhsT=cF["oh"][0:12, (h_ * 3 + br) * 64:(h_ * 3 + br + 1) * 64],
                                         rhs=gT[0:12, tile_sl(t)], start=True, stop=True), reads=[gT, cF["oh"]], writes=[pg])
            return pg

        kb.op(pe, lambda h: h.matmul(bank_imp[:, :], lhsT=zer_b[:, 0:128], rhs=zer_b[:, :], start=True, stop=False),
              reads=[zer_b], writes=[bank_imp])
        for h_ in range(4):
            for j in range(NT):
                pss = rotA()
                kb.op(pe, lambda h: h.matmul(pss[0:127, :], lhsT=kcT[0:64, 0:127], rhs=qn[h_][0:64, tile_sl(j)], start=True, stop=True),
                      reads=[kcT, qn[h_]], writes=[pss])
                P = nextP()
                kb.op(act, lambda h: h.activation(out=P[0:127, :], in_=pss[0:127, :], func=AF.Exp, scale=0.125), reads=[pss], writes=[P])
                kb.op(dve, lambda h: h.tensor_tensor(out=P[0:127, :], in0=P[0:127, :], in1=cF["cmpmask"][0:127, tile_sl(j)], op=ALU.mult),
                      reads=[P, cF["cmpmask"]], writes=[P])
                pr = rotA()
                kb.op(pe, lambda h: h.matmul(pr[:, :], lhsT=ones_b[0:127, :], rhs=P[0:127, :], start=True, stop=True),
                      reads=[P, ones_b], writes=[pr])
                r = tf()
                kb.op(dve, lambda h: h.tensor_scalar(out=r[:, :], in0=pr[:, :], scalar1=1e-30, scalar2=None, op0=ALU.max), reads=[pr], writes=[r])
                recip_act(r[:, :], [r])
                Pn = nextP()
                kb.op(dve, lambda h: h.tensor_tensor(out=Pn[0:127, :], in0=P[0:127, :], in1=r[0:127, :], op=ALU.mult), reads=[P, r], writes=[Pn])
                po = rotA()
                kb.op(pe, lambda h: h.matmul(po[64:128, :], lhsT=vca[0:127, 0:64], rhs=Pn[0:127, :], start=True, stop=True),
                      reads=[Pn, vca], writes=[po])
                for qb in range(4):
                    b = 4 * j + qb
                    kb.op(pe, lambda h: h.matmul(bank_imp[:, b * 32:(b + 1) * 32], lhsT=Pn[0:127, qb * 128:(qb + 1) * 128],
                                                 rhs=cF["mcs"][0:127, :], start=False, stop=(h_ == 3 and b == NB - 1)),
                          reads=[Pn, cF["mcs"]], writes=[bank_imp])
                pg = grep_(h_, 0, j)
                oc = tf()
                kb.op(dve, lambda h: h.tensor_copy(out=oc[64:128, :], in_=po[64:128, :]), reads=[po], writes=[oc])
                kb.op(dve, lambda h: h.tensor_tensor(out=concat[(h_ % 2) * 64:(h_ % 2) * 64 + 64, h_ // 2, tile_sl(j)],
                                                     in0=oc[64:128, :], in1=pg[64:128, :], op=ALU.mult), reads=[oc, pg], writes=[concat])
        mark("n_cmpat")
        for b in range(8, NB):
            sc = tf()
            kb.op(dve, lambda h: h.tensor_tensor(out=sc[:, 0:32], in0=bank_imp[:, b * 32:(b + 1) * 32], in1=cF["keep"][:, b * 32:(b + 1) * 32],
                                                 op=ALU.mult), reads=[bank_imp, cF["keep"]], writes=[sc])
            kb.op(dve, lambda h: h.tensor_tensor(out=sc[:, 0:32], in0=sc[:, 0:32], in1=cF["addt"][:, b * 32:(b + 1) * 32], op=ALU.add),
                  reads=[sc, cF["addt"]], writes=[sc])
            kb.op(dve, lambda h: h.max(out=sc[:, 64:72], in_=sc[:, 0:32]), reads=[sc], writes=[sc])
            kb.op(dve, lambda h: h.match_replace(out=sc[:, 32:64], in_to_replace=sc[:, 64:72], in_values=sc[:, 0:32], imm_value=-1e30),
                  reads=[sc], writes=[sc])
            kb.op(dve, lambda h: h.max(out=sc[:, 72:80], in_=sc[:, 32:64]), reads=[sc], writes=[sc])
            kb.op(dve, lambda h: h.tensor_scalar(out=sc[:, 128:160], in0=sc[:, 0:32], scalar1=sc[:, 79:80], scalar2=None, op0=ALU.is_ge),
                  reads=[sc], writes=[sc])
            kb.op(dve, lambda h: h.tensor_scalar(out=sc[:, 128:160], in0=sc[:, 128:160], scalar1=30000.0, scalar2=-30000.0,
                                                 op0=ALU.mult, op1=ALU.add), reads=[sc], writes=[sc])
            pt_ = rotA()
            kb.op(pe, lambda h: h.transpose(pt_[0:32, 0:128], sc[:, 128:160], cF["ident"][:]), reads=[sc, cF["ident"]], writes=[pt_])
            kb.op(act, lambda h: h.copy(selT[0:32, (b - 8) * 128:(b - 7) * 128], pt_[0:32, 0:128]), reads=[pt_], writes=[selT])

        mark("n_topk")

        def slc_smask(kt, j, c0, c1):
            if j < 2:
                return None
            return (cF["emat"][0:128, kt * 128:(kt + 1) * 128],
                    selT[0:128, (j - 2) * 512 + c0:(j - 2) * 512 + c1], [selT, cF["emat"]])

        attn["on"] = True
        attn["imp"] = True
        for h_ in range(4):
            for j in range(NT):
                for br, (kt_, struct, vi, smk) in enumerate([(kslc, causal_struct, 0, slc_smask), (kwin, window_struct, 1, None)]):
                    acc = attend(j, struct, qn[h_][0:128, :], [qn[h_]], lambda kt: kt_[0:128, kt * 128:(kt + 1) * 128], [kt_],
                                 lambda kt: 128, lambda kt: vaug[:, kt, vi, :], [vaug], 0.125, smask=smk)
                    pg = grep_(h_, br + 1, j)
                    ob = tf()
                    gsb = tf()
                    kb.op(act, lambda h: h.copy(gsb[64:128, :], pg[64:128, :]), reads=[pg], writes=[gsb])
                    p0 = (h_ % 2) * 64
                    norm_out(acc, ob[p0:p0 + 64, :], [ob], mul_ap=gsb[64:128, :], mul_bufs=[gsb])
                    dst = concat[p0:p0 + 64, h_ // 2, tile_sl(j)]
                    kb.op(dve, lambda h: h.tensor_tensor(out=dst, in0=dst, in1=ob[p0:p0 + 64, :], op=ALU.add), reads=[ob, concat], writes=[concat])

    MIXER["fn"] = mixer

    marks = []
    kb.marks = marks

    def mark(name):
        marks.append((name, kb.pe.cnt))

    def run():
        for si in range(nseq):
            if "noload" not in SKIP:
                load_x(si)
            else:
                x32[0] = big.view("x32", 0, (KC, S), F32)
                kb.op(dve, lambda h: h.memset(x32[0][:, 0, :], 1.0), writes=[x32[0]])
            if stop_after == "load":
                store_out(si, 1.0 / ALPHA)
                return
            if stop_after == "ffnonly":
                ffn(ffn_chunks(0, 1))
                store_out(si, 1.0 / ALPHA)
                return
            for l in range(depth):
                last = (l == depth - 1)
                mark("start")
                ffn(MIXER.pop("ffn1_ids") if "ffn1_ids" in MIXER else ffn_chunks(l, 1))
                mark("ffn1")
                layernorm(l, 0, True)
                mark("ln1")
                if stop_after == "ffn1":
                    store_out(si, 1.0 / ALPHA)
                    return
                if "fn" in MIXER:
                    MIXER["fn"](l, si)
                    if stop_after == "mixer":
                        store_out(si, 1.0 / ALPHA)
                        return
                mark("mixer")
                ffn(MIXER.pop("ffn2_ids") if "ffn2_ids" in MIXER else ffn_chunks(l, 2))
                mark("ffn2")
                layernorm(l, 2, False)
                mark("ln3")
                nxt = (l + 1) if not last else (0 if si + 1 < nseq else None)
                ple(l, si, last, nxt)
                mark("ple")
            store_out(si)
            mark("store")

    print('sbuf bytes remaining', nc.sbuf_bytes_remaining)
    return nc, es, kb, run, locals()


def finish(nc, es, kb):
    for tok in kb.out_toks:
        kb.wait_tok(kb.sp, tok)
    es.close()
    return nc


def build_program(nseq=SEQ_PER_CORE, depth=DEPTH, stop_after=None):
    nc, es, kb, run, env = build(nseq, depth, stop_after)
    add_mixer(env)
    run()
    return finish(nc, es, kb)


def add_mixer(env):
    pass


_CONSTS = None


def kernel(**inputs):
    global _CONSTS
    if _CONSTS is None:
        _CONSTS = make_consts()
    nc = build_program()
    x = np.ascontiguousarray(inputs["x"], dtype=np.float32)
    p = np.ascontiguousarray(inputs["p"], dtype=np.float32)
    in_maps = []
    for c in range(NCORES):
        m = {"x": x[c * SEQ_PER_CORE:(c + 1) * SEQ_PER_CORE],
             "p": np.ascontiguousarray(p[:, c * SEQ_PER_CORE:(c + 1) * SEQ_PER_CORE])}
        for n in WEIGHT_SHAPES:
            m[n] = np.ascontiguousarray(inputs[n], dtype=np.float32)
        for n in CONST_SHAPES:
            m["c_" + n] = _CONSTS[n]
        in_maps.append(m)
    res = run_bass_kernel_spmd(nc, in_maps, core_ids=list(range(NCORES)))
    return np.concatenate([r["out"] for r in res.results], axis=0)
```
